# Optimizing a Trainium2 kernel written in Bass

```python
import jax, jax.numpy as jnp
from jax import lax
import numpy as np

D_MODEL = 2048
BATCH = 4
SEQ = 8192
DEPTH = 2
DEC_BATCH = 8
DEC_SEQ = 64
PAST_LEN = 1024

CHUNK = 64
N_EVEN = (DEPTH + 1) // 2
N_ODD = DEPTH // 2
D_FF = 5632
A_WIDTH = D_MODEL // 2
A_GROUPS = 4
A_GROUP_DIM = A_WIDTH // A_GROUPS
A_CHUNK = 128
B_WIDTH = D_MODEL // 2
B_CONV_WIDTH = 31
B_HIST = B_CONV_WIDTH - 1
C_WINDOWS = (2, 4, 8, 16)
C_GROUPS = 4
C_GROUP_DIM = D_MODEL // C_GROUPS
C_HIST = max(C_WINDOWS) - 1
MIX_IN = 2 * A_WIDTH + 2 * B_WIDTH
MIX_OUT = A_WIDTH + B_WIDTH
EPS = 1e-6

kernel_name = 'chunk_causal_gmlp_conformer_pool_stream_step'


def rms_norm(x, g):
    xf = x.astype(jnp.float32)
    y = xf * lax.rsqrt(jnp.mean(xf * xf, axis=-1, keepdims=True) + EPS)
    return (y * g.astype(jnp.float32)).astype(x.dtype)


def layer_norm(x, g, b):
    xf = x.astype(jnp.float32)
    mu = jnp.mean(xf, axis=-1, keepdims=True)
    d = xf - mu
    var = jnp.mean(d * d, axis=-1, keepdims=True)
    y = d * lax.rsqrt(var + EPS) * g.astype(jnp.float32) + b.astype(jnp.float32)
    return y.astype(x.dtype)


def modulate(xn, shift, scale):
    return xn * (1 + scale[:, None, :]) + shift[:, None, :]


def swiglu(x, wg, wu, wd):
    return (jax.nn.silu(x @ wg) * (x @ wu)) @ wd


def chunk_mask():
    i = jnp.arange(A_CHUNK)
    return (i[None, :] // CHUNK) <= (i[:, None] // CHUNK)


def spatial_gating(vn, ws, bs):
    w = jnp.where(chunk_mask()[None], ws, 0)
    bsz, T = vn.shape[0], vn.shape[1]
    if T >= A_CHUNK:
        vc = vn.reshape(bsz, T // A_CHUNK, A_CHUNK, A_GROUPS, A_GROUP_DIM)
        out = jnp.einsum('gij,bnjgc->bnigc', w, vc) + bs.T[None, None, :, :, None]
        return out.reshape(bsz, T, A_GROUPS, A_GROUP_DIM)
    return jnp.einsum('gij,bjgc->bigc', w[:, :T, :T], vn) + bs[:, :T].T[None, :, :, None]


def mixer_ab(hm, conv_hist, w_in, v_g, v_b, ws, bs, dw, ln_g, ln_b, w_out):
    bsz, T, _ = hm.shape
    proj = hm @ w_in
    a_u, a_v = jnp.split(jax.nn.gelu(proj[..., :2 * A_WIDTH]), 2, axis=-1)
    b_a, b_g = jnp.split(proj[..., 2 * A_WIDTH:], 2, axis=-1)
    vn = layer_norm(a_v, v_g, v_b)
    gate = spatial_gating(vn.reshape(bsz, T, A_GROUPS, A_GROUP_DIM), ws, bs)
    a_out = a_u * gate.reshape(bsz, T, A_WIDTH)
    glu = b_a * jax.nn.sigmoid(b_g)
    xpad = jnp.concatenate([conv_hist.astype(glu.dtype), glu], axis=1)
    conv = lax.conv_general_dilated(
        xpad, dw[:, None, :].astype(xpad.dtype), window_strides=(1,), padding='VALID',
        dimension_numbers=('NWC', 'WIO', 'NWC'), feature_group_count=B_WIDTH)
    b_out = jax.nn.silu(layer_norm(conv, ln_g, ln_b))
    y = jnp.concatenate([a_out, b_out], axis=-1) @ w_out
    return y, vn, xpad[:, -B_HIST:]


def mixer_c(hm, pool_hist, start_pos, w_grp, b_grp, scale):
    bsz, T, _ = hm.shape
    xpad = jnp.concatenate([pool_hist.astype(hm.dtype), hm], axis=1)
    csum = jnp.concatenate([jnp.zeros((bsz, 1, D_MODEL), jnp.float32),
                            jnp.cumsum(xpad.astype(jnp.float32), axis=1)], axis=1)
    end = csum[:, C_HIST + 1:]
    pos = start_pos + jnp.arange(T)
    pooled = []
    for g, w in enumerate(C_WINDOWS):
        sl = slice(g * C_GROUP_DIM, (g + 1) * C_GROUP_DIM)
        s = end[..., sl] - csum[:, C_HIST + 1 - w:C_HIST + 1 - w + T, sl]
        cnt = jnp.minimum(w, pos + 1).astype(jnp.float32)
        pooled.append(s / cnt[None, :, None])
    pooled = (jnp.concatenate(pooled, axis=-1) - hm.astype(jnp.float32)).astype(hm.dtype)
    pg = pooled.reshape(bsz, T, C_GROUPS, C_GROUP_DIM)
    y = jnp.einsum('btgc,gcd->btgd', pg, w_grp) + b_grp
    return y.reshape(bsz, T, D_MODEL) * scale, xpad[:, -C_HIST:]


def run_trunk(x, c, start_pos, conv_hist, pool_hist, p):
    h = x
    new_conv, new_pool, new_av = [], [], []
    for l in range(DEPTH):
        mod = (jax.nn.silu(c) @ p['ada_w'][l] + p['ada_b'][l]).reshape(c.shape[0], 3, 3, D_MODEL)
        hn = modulate(rms_norm(h, p['norm_g'][l, 0]), mod[:, 0, 0], mod[:, 0, 1])
        h = h + 0.5 * mod[:, 0, 2][:, None] * swiglu(
            hn, p['ffn_w_gate'][l, 0], p['ffn_w_up'][l, 0], p['ffn_w_down'][l, 0])
        hn = modulate(rms_norm(h, p['norm_g'][l, 1]), mod[:, 1, 0], mod[:, 1, 1])
        if l % 2 == 0:
            e = l // 2
            y, vn, ch = mixer_ab(hn, conv_hist[e], p['ab_w_in'][e], p['a_v_norm_g'][e],
                                 p['a_v_norm_b'][e], p['a_ws'][e], p['a_bs'][e], p['b_dw'][e],
                                 p['b_ln_g'][e], p['b_ln_b'][e], p['ab_w_out'][e])
            new_conv.append(ch)
            new_av.append(vn)
        else:
            o = l // 2
            y, ph = mixer_c(hn, pool_hist[o], start_pos, p['c_w_grp'][o], p['c_b_grp'][o],
                            p['c_scale'][o])
            new_pool.append(ph)
        h = h + mod[:, 1, 2][:, None] * y
        hn = modulate(rms_norm(h, p['norm_g'][l, 2]), mod[:, 2, 0], mod[:, 2, 1])
        h = h + 0.5 * mod[:, 2, 2][:, None] * swiglu(
            hn, p['ffn_w_gate'][l, 1], p['ffn_w_up'][l, 1], p['ffn_w_down'][l, 1])
    return rms_norm(h, p['final_norm_g']), jnp.stack(new_conv), jnp.stack(new_pool), jnp.stack(new_av)


def setup_inputs(seed: int = 0) -> dict:
    key = jax.random.key(seed)
    ks = jax.random.split(key, 25)
    f32 = jnp.float32

    def nrm(k, shape, s):
        return jax.random.normal(k, shape, f32) * s

    ada_b = nrm(ks[7], (DEPTH, 3, 3, D_MODEL), 0.02).at[:, :, 2].add(1.0).reshape(DEPTH, 9 * D_MODEL)
    return {
        'x_prompt': nrm(ks[0], (BATCH, SEQ, D_MODEL), 1.0),
        'x_sample': nrm(ks[1], (DEC_BATCH, DEC_SEQ, D_MODEL), 1.0),
        'c_prompt': nrm(ks[2], (BATCH, D_MODEL), 1.0),
        'c_sample': nrm(ks[3], (DEC_BATCH, D_MODEL), 1.0),
        'state_conv_b': nrm(ks[4], (N_EVEN, DEC_BATCH, B_HIST, B_WIDTH), 0.5),
        'state_pool_c': nrm(ks[5], (N_ODD, DEC_BATCH, C_HIST, D_MODEL), 1.0),
        'ada_w': nrm(ks[6], (DEPTH, D_MODEL, 9 * D_MODEL), 0.25 * D_MODEL ** -0.5),
        'ada_b': ada_b,
        'norm_g': 1.0 + nrm(ks[8], (DEPTH, 3, D_MODEL), 0.02),
        'final_norm_g': 1.0 + nrm(ks[9], (D_MODEL,), 0.02),
        'ffn_w_gate': nrm(ks[10], (DEPTH, 2, D_MODEL, D_FF), D_MODEL ** -0.5),
        'ffn_w_up': nrm(ks[11], (DEPTH, 2, D_MODEL, D_FF), D_MODEL ** -0.5),
        'ffn_w_down': nrm(ks[12], (DEPTH, 2, D_FF, D_MODEL), D_FF ** -0.5),
        'ab_w_in': nrm(ks[13], (N_EVEN, D_MODEL, MIX_IN), D_MODEL ** -0.5),
        'a_v_norm_g': 1.0 + nrm(ks[14], (N_EVEN, A_WIDTH), 0.02),
        'a_v_norm_b': nrm(ks[15], (N_EVEN, A_WIDTH), 0.02),
        'a_ws': nrm(ks[16], (N_EVEN, A_GROUPS, A_CHUNK, A_CHUNK), A_CHUNK ** -0.5),
        'a_bs': 1.0 + nrm(ks[17], (N_EVEN, A_GROUPS, A_CHUNK), 0.02),
        'b_dw': nrm(ks[18], (N_EVEN, B_CONV_WIDTH, B_WIDTH), B_CONV_WIDTH ** -0.5),
        'b_ln_g': 1.0 + nrm(ks[19], (N_EVEN, B_WIDTH), 0.02),
        'b_ln_b': nrm(ks[20], (N_EVEN, B_WIDTH), 0.02),
        'ab_w_out': nrm(ks[21], (N_EVEN, MIX_OUT, D_MODEL), MIX_OUT ** -0.5),
        'c_w_grp': nrm(ks[22], (N_ODD, C_GROUPS, C_GROUP_DIM, C_GROUP_DIM), C_GROUP_DIM ** -0.5),
        'c_b_grp': nrm(ks[23], (N_ODD, C_GROUPS, C_GROUP_DIM), 0.02),
        'c_scale': 1.0 + nrm(ks[24], (N_ODD, D_MODEL), 0.02),
    }


def reference(x_prompt, x_sample, c_prompt, c_sample, state_conv_b, state_pool_c,
              ada_w, ada_b, norm_g, final_norm_g, ffn_w_gate, ffn_w_up, ffn_w_down,
              ab_w_in, a_v_norm_g, a_v_norm_b, a_ws, a_bs, b_dw, b_ln_g, b_ln_b, ab_w_out,
              c_w_grp, c_b_grp, c_scale):
    p = {
        'ada_w': ada_w, 'ada_b': ada_b, 'norm_g': norm_g, 'final_norm_g': final_norm_g,
        'ffn_w_gate': ffn_w_gate, 'ffn_w_up': ffn_w_up, 'ffn_w_down': ffn_w_down,
        'ab_w_in': ab_w_in, 'a_v_norm_g': a_v_norm_g, 'a_v_norm_b': a_v_norm_b,
        'a_ws': a_ws, 'a_bs': a_bs, 'b_dw': b_dw, 'b_ln_g': b_ln_g, 'b_ln_b': b_ln_b,
        'ab_w_out': ab_w_out, 'c_w_grp': c_w_grp, 'c_b_grp': c_b_grp, 'c_scale': c_scale,
    }
    bp = x_prompt.shape[0]
    conv0 = jnp.zeros((N_EVEN, bp, B_HIST, B_WIDTH), x_prompt.dtype)
    pool0 = jnp.zeros((N_ODD, bp, C_HIST, D_MODEL), x_prompt.dtype)
    y_prompt, new_conv_prompt, new_pool_prompt, _ = run_trunk(
        x_prompt, c_prompt, 0, conv0, pool0, p)
    y_sample, new_conv_sample, new_pool_sample, new_av_sample = run_trunk(
        x_sample, c_sample, PAST_LEN, state_conv_b, state_pool_c, p)
    return (y_prompt, y_sample, new_conv_prompt, new_conv_sample,
            new_pool_prompt, new_pool_sample, new_av_sample)
```

```python
import numpy as np
from contextlib import ExitStack
import concourse.bass as bass
import concourse.mybir as mybir
from concourse.bass_utils import run_bass_kernel_spmd

F32 = mybir.dt.float32
BF16 = mybir.dt.bfloat16
AF = mybir.ActivationFunctionType
ALU = mybir.AluOpType
AX = mybir.AxisListType

D = 2048
DFF = 5632
KC = 16
MC = 44
EPS = 1e-6
NCORES = 8
SLOT_W = 4096
NS = 4
BIGW = 13376

PK = {}
_o = 0
for _n, _w in [("ada_b", 288), ("norm_g", 96), ("final_g", 16), ("dw", 248), ("ln_g", 8), ("ln_b", 8),
               ("c_b", 16), ("c_scale", 16), ("c2T", 32), ("sconv", 240), ("spool", 240), ("flag", 1),
               ("invc", 256)]:
    PK[_n] = (_o, _w)
    _o += _w
NPK = _o


class Prog:
    ENGS = ("pe", "act", "dve", "pool", "sp")

    def __init__(self):
        self.q = {e: [] for e in self.ENGS}
        self.cnt = {}
        self.keys = {}
        self.big_base = {}

    def _entry(self, k):
        e = self.keys.get(k)
        if e is None:
            base = dict(self.big_base) if (isinstance(k[0], str) and k[0].startswith("B:")) else {}
            e = [None, base]
            self.keys[k] = e
        return e

    def fence_big(self):
        merged = dict(self.big_base)
        for k in [k for k in self.keys if isinstance(k[0], str) and k[0].startswith("B:")]:
            w, rd = self.keys.pop(k)
            if w is not None:
                merged[w[0]] = max(merged.get(w[0], 0), w[1])
            for s, v in rd.items():
                merged[s] = max(merged.get(s, 0), v)
        self.big_base = merged

    def _record(self, eng, fn, reads, writes, sem, inc):
        waits = {}

        def need(tok):
            if tok is not None:
                waits[tok[0]] = max(waits.get(tok[0], 0), tok[1])

        for k in reads:
            need(self._entry(k)[0])
        for k in writes:
            e = self._entry(k)
            need(e[0])
            for s, v in e[1].items():
                need((s, v))
        if eng == "pe":
            waits.pop("pe", None)
        self.cnt[sem] = self.cnt.get(sem, 0) + inc
        tok = (sem, self.cnt[sem])
        self.q[eng].append((fn, waits, (sem, inc)))
        for k in reads:
            e = self._entry(k)
            e[1][sem] = max(e[1].get(sem, 0), tok[1])
        for k in writes:
            self.keys[k] = [tok, {}]
        return tok

    def op(self, eng, fn, reads=(), writes=()):
        return self._record(eng, fn, reads, writes, eng, 1)

    def pe(self, fn, reads=(), writes=()):
        return self.op("pe", fn, reads, writes)

    def act(self, fn, reads=(), writes=()):
        return self.op("act", fn, reads, writes)

    def dve(self, fn, reads=(), writes=()):
        return self.op("dve", fn, reads, writes)

    def pool(self, fn, reads=(), writes=()):
        return self.op("pool", fn, reads, writes)

    def dma(self, queue, out, in_, sem, reads=(), writes=()):
        return self._record(queue, lambda e: e.dma_start(out=out, in_=in_), reads, writes, sem, 16)

    def emit(self, nc, block, sems):
        prog = self

        def run(eng_name, eng):
            waited = {}
            for fn, waits, (sem, inc) in prog.q[eng_name]:
                for s in sorted(waits):
                    v = waits[s]
                    if waited.get(s, 0) < v:
                        eng.wait_ge(sems[s], v)
                        waited[s] = v
                ins = fn(eng)
                ins.then_inc(sems[sem], inc)
            if eng_name == "sp":
                for s, v in prog.cnt.items():
                    if s not in prog.ENGS and waited.get(s, 0) < v:
                        eng.wait_ge(sems[s], v)

        @block.tensor
        def _(t):
            run("pe", t)

        @block.scalar
        def _(a):
            run("act", a)

        @block.vector
        def _(v):
            run("dve", v)

        @block.gpsimd
        def _(g):
            run("pool", g)

        @block.sync
        def _(s):
            run("sp", s)


class Rot:
    def __init__(self, name, aps):
        self.aps = aps
        self.name = name
        self.i = 0

    def next(self):
        i = self.i % len(self.aps)
        self.i += 1
        return self.aps[i], (self.name, i)


def build(NT):
    TOK = NT * 512
    nc = bass.Bass("TRN2", target_bir_lowering=False)
    P = Prog()

    def din(name, shape):
        return nc.dram_tensor(name, list(shape), F32, kind="ExternalInput").ap()

    def dout(name, shape):
        return nc.dram_tensor(name, list(shape), F32, kind="ExternalOutput").ap()

    xp = din("xp", [TOK, D]); xh = din("xh", [128, D]); xsm = din("xsm", [64, D])
    pk_d = din("pk", [128, NPK]); rows_d = din("rows", [1, 2048]); abs_d = din("abs", [1, 512])
    aws_d = din("aws", [4, 128, 128])
    import os as _os
    TINYW = _os.environ.get("MK_TINYW", "") == "1"
    if not TINYW:
        ada_w = din("ada_w", [2, D, 9 * D])
        wg = din("wg", [2, 2, D, DFF]); wu = din("wu", [2, 2, D, DFF]); wd = din("wd", [2, 2, DFF, D])
        win = din("win", [1, D, 4096]); wout = din("wout", [1, D, D]); cw = din("cw", [1, 4, 512, 512])
    else:
        wsm = din("wsm", [D, DFF])
    yp = dout("yp", [TOK, D]); ysm = dout("ysm", [64, D])
    ncp = dout("ncp", [30, 1024]); ncs = dout("ncs", [30, 1024])
    npp = dout("npp", [15, D]); nps = dout("nps", [15, D]); nav = dout("nav", [64, 1024])

    with ExitStack() as es:
        def sb(name, shape, dt=F32):
            return es.enter_context(nc.sbuf_tensor(name, list(shape), dt))

        h_t = sb("h", [128, KC, 512]); h = h_t[:]
        xn_t = sb("xn", [128, KC, 512], BF16); xn = xn_t[:]
        big_t = sb("big", [128, BIGW])
        slots = [sb(f"slot{i}", [128, SLOT_W]) for i in range(NS)]
        xs = Rot("xs", [sb(f"xs{i}", [128, 512])[:] for i in range(2)])
        TS = Rot("ts", [sb(f"ts{i}", [128, 512])[:] for i in range(2)])
        TM = Rot("tm", [sb(f"tmm{i}", [128, 512])[:] for i in range(2)])
        sqs = sb("sqs", [128, 512])[:]
        rstd = sb("rstd", [128, 512])[:]
        pk = sb("pkt", [128, NPK])[:]
        vgb = sb("vgb", [128, 2048])[:]
        abs_t = sb("abst", [1, 512])[:]
        ident = sb("ident", [128, 128])[:]
        ones_f = sb("onesf", [128, 128])[:]
        eps_t = sb("epst", [128, 1])[:]
        WmT = sb("WmT", [128, 4, 128], BF16)[:]
        sc = sb("sc", [128, 32], BF16)[:]
        ssq = sb("ssq", [128, 512])[:]
        mod = [sb(f"mod{l}", [128, 144, 2])[:] for l in range(2)]
        Aall = sb("Aall", [128, 12, 16])[:]
        Gall = sb("Gall", [128, 12, 16])[:]
        sgall = sb("sgall", [128, 2, 16])[:]
        bsgall = sb("bsgall", [128, 2, 16])[:]
        convhist = [sb(f"convhist{s}", [128, 8, 30])[:] for s in range(2)]
        poolhist = [sb(f"poolhist{s}", [128, 16, 15])[:] for s in range(2)]
        banks = [es.enter_context(nc.psum_tensor(f"bank{i}", [128, 512], F32))[:] for i in range(8)]
        PS = Rot("ps", banks)

        big = big_t[:]

        def bview(off, words, dt=F32, pattern=None, **kw):
            v = big[:, off:off + words]
            if dt == BF16:
                v = v.bitcast(BF16)
            if pattern:
                v = v.rearrange(pattern, **kw)
            return v

        act = bview(0, 11264, BF16, "p (m t) -> p m t", m=MC)
        sq = bview(0, 8192, F32, "p (k t) -> p k t", k=KC)
        vg = Rot("B:vg", [bview(0, 1024), bview(1024, 1024)])
        vnb = bview(2048, 2048, BF16, "p (b c) -> p b c", b=4)
        glu = Rot("B:glu", [bview(4096, 544), bview(4640, 544)])
        acc = bview(5184, 4096, F32, "p (c t) -> p c t", c=8)
        ab = bview(9280, 4096, BF16, "p (k t) -> p k t", k=16)
        HMG = Rot("B:hmg", [bview(0, 2112, F32, "p (j t) -> p j t", j=4), bview(2112, 2112, F32, "p (j t) -> p j t", j=4)])
        sA = bview(4224, 2112, F32, "p (j t) -> p j t", j=4)
        sB = bview(6336, 2112, F32, "p (j t) -> p j t", j=4)
        PL = Rot("B:pl", [bview(8448, 1024, BF16, "p (j t) -> p j t", j=4), bview(9472, 1024, BF16, "p (j t) -> p j t", j=4)])
        yn = bview(0, 8192, F32, "p (k t) -> p k t", k=KC)
        YS = Rot("B:ys", [bview(8192, 1024), bview(9216, 1024)])
        ncst = bview(10240, 1024)
        npst = bview(11264, 2048)
        aws_st = bview(0, 512, F32, "p (g j) -> p g j", g=4)

        def pkv(name, a=0, b=None):
            o, w = PK[name]
            b = w if b is None else b
            return pk[:, o + a:o + b]

        ring_i = [0]

        def wget(src, shape, dt):
            s = ring_i[0] % NS
            ring_i[0] += 1
            words = int(np.prod(shape[1:])) // (2 if dt == BF16 else 1)
            v = slots[s][:, 0:words]
            if dt == BF16:
                v = v.bitcast(BF16)
            if len(shape) == 3:
                v = v.rearrange("p (a b) -> p a b", a=shape[1])
            key = ("slot", s)
            P.dma("pool", v, src, f"w{s}", reads=(), writes=(key,))
            return v, key

        if not TINYW:
            wg_v = [[wg[l, f].rearrange("(kc p) n -> p kc n", p=128) for f in range(2)] for l in range(2)]
            wu_v = [[wu[l, f].rearrange("(kc p) n -> p kc n", p=128) for f in range(2)] for l in range(2)]
            wd_v = [[wd[l, f].rearrange("(m p) n -> p m n", p=128) for f in range(2)] for l in range(2)]
            win_v = win[0].rearrange("(kc p) n -> p kc n", p=128)
            wout_v = wout[0].rearrange("(kc p) n -> p kc n", p=128)
            cw_v = [cw[0, g].rearrange("(kc p) n -> p kc n", p=128) for g in range(4)]
            ada_v = [ada_w[l].rearrange("(kc p) n -> p kc n", p=128) for l in range(2)]
        else:
            w_kc = wsm.rearrange("(kc p) n -> p kc n", p=128)
            wg_v = [[w_kc for f in range(2)] for l in range(2)]
            wu_v = wg_v
            w_d = wsm.rearrange("a b -> (a b)").rearrange("(r c) -> r c", c=D).rearrange("(m p) n -> p m n", p=128)
            wd_v = [[w_d for f in range(2)] for l in range(2)]
            win_v = w_kc
            wout_v = w_kc
            cw_v = [wsm[0:512, g * 512:(g + 1) * 512].rearrange("(kc p) n -> p kc n", p=128) for g in range(4)]

            class _AdaV:
                def __getitem__(self, idx):
                    sl = idx[2]
                    jb = (sl.start // 512) % 11
                    return w_kc[:, :, jb * 512:(jb + 1) * 512]
            ada_v = [_AdaV(), _AdaV()]

        def mm(t, out, lhsT, rhs, start, stop):
            return t.matmul(out, lhsT, rhs, start=start, stop=stop)

        P.dma("sp", pk, pk_d[:, :], "c0", writes=(("pk",),))
        P.dma("sp", vgb, rows_d[0:1, :].partition_broadcast(128), "c1", writes=(("vgb",),))
        P.dma("sp", abs_t, abs_d[:, :], "c2", writes=(("abs",),))
        P.fence_big()
        P.dma("sp", aws_st, aws_d.rearrange("g i j -> i g j"), "c3", writes=(("B:aws",),))
        P.pool(lambda g: g.memset(ident, 0.0), writes=(("ident",),))
        P.pool(lambda g: g.affine_select(out=ident, in_=ident, compare_op=ALU.not_equal, fill=1.0, base=0,
                                         pattern=[[-1, 128]], channel_multiplier=1),
               reads=(("ident",),), writes=(("ident",),))
        P.dve(lambda v: v.memset(ones_f, 1.0), writes=(("ones",),))
        P.dve(lambda v: v.memset(eps_t, EPS), writes=(("eps",),))
        P.dve(lambda v: v.memset(convhist[0], 0.0), writes=(("ch", 0),))
        P.dve(lambda v: v.memset(poolhist[0], 0.0), writes=tuple(("ph", 0, g) for g in range(4)))
        P.dve(lambda v: v.tensor_copy(out=convhist[1], in_=pkv("sconv").rearrange("p (c k) -> p c k", c=8)),
              reads=(("pk",),), writes=(("ch", 1),))
        P.dve(lambda v: v.tensor_copy(out=poolhist[1], in_=pkv("spool").rearrange("p (c k) -> p c k", c=16)),
              reads=(("pk",),), writes=tuple(("ph", 1, g) for g in range(4)))
        bk, bkey = PS.next()
        P.pe(lambda t: [t.transpose(bk[:, g * 128:(g + 1) * 128], aws_st[:, g, :], ident) for g in range(4)][-1],
             reads=(("B:aws",), ("ident",)), writes=(bkey,))
        P.act(lambda a: a.copy(out=WmT, in_=bk.rearrange("p (g i) -> p g i", g=4)), reads=(bkey,), writes=(("WmT",),))
        P.dve(lambda v: v.memset(WmT[64:128, :, 0:64], 0.0), reads=(("WmT",),), writes=(("WmT",),))
        P.act(lambda a: a.activation(out=sc, in_=pkv("c2T"), func=AF.Silu), reads=(("pk",),), writes=(("sc",),))
        for l in range(2):
            bk, bkey = PS.next()
            for jb in range(36):
                blk, bkk = wget(ada_v[l][:, :, jb * 512:(jb + 1) * 512], [128, 16, 512], BF16)

                def fn(t, blk=blk, bk=bk, jb=jb):
                    ins = None
                    for j in range(4):
                        q = jb * 4 + j
                        for kc in range(KC):
                            ins = mm(t, bk[:, q * 2:(q + 1) * 2], blk[:, kc, j * 128:(j + 1) * 128],
                                     sc[:, kc * 2:(kc + 1) * 2], kc == 0, kc == KC - 1)
                    return ins
                P.pe(fn, reads=(bkk, ("sc",)), writes=(bkey,))
            for s in range(2):
                P.dve(lambda v, l=l, s=s, bk=bk: v.tensor_tensor(
                    out=mod[l][:, :, s], in0=bk[:, 0:288].rearrange("p (q s) -> p q s", s=2)[:, :, s],
                    in1=pkv("ada_b", l * 144, (l + 1) * 144), op=ALU.add),
                    reads=(bkey, ("pk",)), writes=(("mod", l),))
        for s in range(2):
            for l in range(2):
                for i in range(3):
                    idx = (s * 2 + l) * 3 + i
                    P.dve(lambda v, s=s, l=l, i=i, idx=idx: v.scalar_tensor_tensor(
                        out=Aall[:, idx, :], in0=mod[l][:, (i * 3 + 1) * 16:(i * 3 + 2) * 16, s], scalar=1.0,
                        in1=pkv("norm_g", (l * 3 + i) * 16, (l * 3 + i + 1) * 16), op0=ALU.add, op1=ALU.mult),
                        reads=(("mod", l), ("pk",)), writes=(("consts",),))
                    P.dve(lambda v, s=s, l=l, i=i, idx=idx: v.tensor_scalar(
                        out=Gall[:, idx, :], in0=mod[l][:, (i * 3 + 2) * 16:(i * 3 + 3) * 16, s],
                        scalar1=(1.0 if i == 1 else 0.5), scalar2=None, op0=ALU.mult),
                        reads=(("mod", l),), writes=(("consts",),))
            P.dve(lambda v, s=s: v.tensor_tensor(out=sgall[:, s, :], in0=Gall[:, (s * 2 + 1) * 3 + 1, :],
                                                 in1=pkv("c_scale"), op=ALU.mult),
                  reads=(("consts",), ("pk",)), writes=(("consts",),))
            P.dve(lambda v, s=s: v.tensor_tensor(out=bsgall[:, s, :], in0=sgall[:, s, :], in1=pkv("c_b"), op=ALU.mult),
                  reads=(("consts",), ("pk",)), writes=(("consts",),))

        HK = [("h", dc) for dc in range(KC)]
        XNK = [("xn", kc) for kc in range(KC)]

        def load_x(tl):
            T = tl["T"]
            TB = min(T, 128)
            for tb in range(T // TB):
                for q in range(4):
                    xb, xk = xs.next()
                    P.dma("sp", xb[0:TB, :], tl["x"][tb * TB:(tb + 1) * TB, q * 512:(q + 1) * 512], f"x{xk[1]}",
                          writes=(xk,))
                    bk, bkey = PS.next()
                    P.pe(lambda t, xb=xb, bk=bk, TB=TB: [
                        t.transpose(bk[:, j * TB:(j + 1) * TB], xb[0:TB, j * 128:(j + 1) * 128], ident[0:TB, 0:TB])
                        for j in range(4)][-1], reads=(xk, ("ident",)), writes=(bkey,))
                    P.act(lambda a, bk=bk, q=q, tb=tb, TB=TB: a.copy(
                        out=h[:, 4 * q:4 * q + 4, tb * TB:(tb + 1) * TB],
                        in_=bk[:, 0:4 * TB].rearrange("p (j t) -> p j t", j=4)),
                        reads=(bkey,), writes=tuple(("h", 4 * q + j) for j in range(4)))

        def norm_stats(tl):
            T = tl["T"]
            P.fence_big()
            P.act(lambda a: a.activation(out=sq[:, :, :T], in_=h[:, :, :T], func=AF.Square), reads=HK, writes=(("B:sq",),))
            P.dve(lambda v: v.tensor_reduce(out=sqs[:, :T], in_=sq[:, :, :T].rearrange("p k t -> p t k"), axis=AX.X,
                                            op=ALU.add), reads=(("B:sq",),), writes=(("sqs",),))
            bk, bkey = PS.next()
            P.pe(lambda t: mm(t, bk[:, :T], ones_f, sqs[:, :T], True, True), reads=(("sqs",), ("ones",)), writes=(bkey,))
            P.act(lambda a: a.activation(out=sqs[:, :T], in_=bk[:, :T], func=AF.Sqrt, scale=1.0 / D, bias=eps_t[:, 0:1]),
                  reads=(bkey, ("eps",)), writes=(("sqs",),))
            P.dve(lambda v: v.reciprocal(out=rstd[:, :T], in_=sqs[:, :T]), reads=(("sqs",),), writes=(("rstd",),))

        def ssq_update(tl, dc):
            T = tl["T"]
            ts, tsk = TS.next()
            P.act(lambda a, ts=ts, dc=dc: a.activation(out=ts[:, :T], in_=h[:, dc, :T], func=AF.Square),
                  reads=(("h", dc),), writes=(tsk,))
            if tl["ssq_n"] == 0:
                P.dve(lambda v, ts=ts: v.tensor_copy(out=ssq[:, :T], in_=ts[:, :T]), reads=(tsk,), writes=(("ssq",),))
            else:
                P.dve(lambda v, ts=ts: v.tensor_tensor(out=ssq[:, :T], in0=ssq[:, :T], in1=ts[:, :T], op=ALU.add),
                      reads=(tsk, ("ssq",)), writes=(("ssq",),))
            tl["ssq_n"] += 1

        def norm_stats_inc(tl):
            T = tl["T"]
            assert tl["ssq_n"] == KC, tl["ssq_n"]
            tl["ssq_n"] = 0
            bk, bkey = PS.next()
            P.pe(lambda t: mm(t, bk[:, :T], ones_f, ssq[:, :T], True, True), reads=(("ssq",), ("ones",)), writes=(bkey,))
            P.act(lambda a: a.activation(out=sqs[:, :T], in_=bk[:, :T], func=AF.Sqrt, scale=1.0 / D, bias=eps_t[:, 0:1]),
                  reads=(bkey, ("eps",)), writes=(("sqs",),))
            P.dve(lambda v: v.reciprocal(out=rstd[:, :T], in_=sqs[:, :T]), reads=(("sqs",),), writes=(("rstd",),))

        def stats(tl):
            if tl["ssq_n"] == KC:
                norm_stats_inc(tl)
            else:
                assert tl["ssq_n"] == 0
                norm_stats(tl)

        def cidx(tl, l, i):
            return (tl["s"] * 2 + l) * 3 + i

        def norm_mod(tl, l, i):
            T = tl["T"]
            s = tl["s"]
            stats(tl)
            ci = cidx(tl, l, i)
            for dc in range(KC):
                tm, tk = TM.next()
                P.dve(lambda v, dc=dc, tm=tm: v.scalar_tensor_tensor(
                    out=tm[:, :T], in0=h[:, dc, :T], scalar=Aall[:, ci, dc:dc + 1], in1=rstd[:, :T],
                    op0=ALU.mult, op1=ALU.mult), reads=(("h", dc), ("rstd",), ("consts",)), writes=(tk,))
                P.act(lambda a, dc=dc, tm=tm: a.activation(
                    out=xn[:, dc, :T], in_=tm[:, :T], func=AF.Identity,
                    bias=mod[l][:, (i * 3) * 16 + dc, s:s + 1], scale=1.0),
                    reads=(tk, ("mod", l)), writes=(("xn", dc),))

        def ffn(tl, l, f):
            T = tl["T"]
            i = 0 if f == 0 else 2
            ci = cidx(tl, l, i)
            norm_mod(tl, l, i)
            P.fence_big()
            for mg in range(11):
                gv, gk = wget(wg_v[l][f][:, :, mg * 512:(mg + 1) * 512], [128, 16, 512], BF16)
                uv, uk = wget(wu_v[l][f][:, :, mg * 512:(mg + 1) * 512], [128, 16, 512], BF16)
                for m4 in range(4):
                    m = mg * 4 + m4
                    bg_, bgk = PS.next()
                    bu_, buk = PS.next()
                    P.pe(lambda t, bg_=bg_, gv=gv, m4=m4: [
                        mm(t, bg_[:, :T], gv[:, kc, m4 * 128:(m4 + 1) * 128], xn[:, kc, :T], kc == 0, kc == KC - 1)
                        for kc in range(KC)][-1], reads=[gk] + XNK, writes=(bgk,))
                    P.pe(lambda t, bu_=bu_, uv=uv, m4=m4: [
                        mm(t, bu_[:, :T], uv[:, kc, m4 * 128:(m4 + 1) * 128], xn[:, kc, :T], kc == 0, kc == KC - 1)
                        for kc in range(KC)][-1], reads=[uk] + XNK, writes=(buk,))
                    ts, tsk = TS.next()
                    P.act(lambda a, ts=ts, bg_=bg_: a.activation(out=ts[:, :T], in_=bg_[:, :T], func=AF.Silu),
                          reads=(bgk,), writes=(tsk,))
                    P.dve(lambda v, ts=ts, bu_=bu_, m=m: v.tensor_tensor(out=act[:, m, :T], in0=bu_[:, :T], in1=ts[:, :T],
                                                                       op=ALU.mult),
                          reads=(buk, tsk), writes=(("B:act", m),))
            for dp in range(8):
                b0, b0k = PS.next()
                b1, b1k = PS.next()
                for mh in range(2):
                    dv, dk = wget(wd_v[l][f][:, mh * 22:(mh + 1) * 22, dp * 256:(dp + 1) * 256], [128, 22, 256], BF16)

                    def fn(t, dv=dv, mh=mh, b0=b0, b1=b1):
                        ins = None
                        for mm_ in range(22):
                            m = mh * 22 + mm_
                            for dci, bb in enumerate((b0, b1)):
                                ins = mm(t, bb[:, :T], dv[:, mm_, dci * 128:(dci + 1) * 128], act[:, m, :T],
                                         m == 0, m == MC - 1)
                        return ins
                    P.pe(fn, reads=[dk] + [("B:act", mh * 22 + k) for k in range(22)], writes=(b0k, b1k))
                for dci, (bb, bbk) in enumerate(((b0, b0k), (b1, b1k))):
                    dc = 2 * dp + dci
                    P.dve(lambda v, bb=bb, dc=dc: v.scalar_tensor_tensor(
                        out=h[:, dc, :T], in0=bb[:, :T], scalar=Gall[:, ci, dc:dc + 1], in1=h[:, dc, :T],
                        op0=ALU.mult, op1=ALU.add), reads=(bbk, ("h", dc), ("consts",)), writes=(("h", dc),))
                    ssq_update(tl, dc)

        def out_state(src, nchunk, nrow, stage, dst, keys):
            for q in range(nchunk // 4):
                bk, bkey = PS.next()
                P.pe(lambda t, bk=bk, q=q: [
                    t.transpose(bk[0:nrow, j * 128:(j + 1) * 128], src[:, 4 * q + j, :], ident) for j in range(4)][-1],
                    reads=list(keys) + [("ident",)], writes=(bkey,))
                P.act(lambda a, bk=bk, q=q: a.copy(out=stage[0:nrow, q * 512:(q + 1) * 512], in_=bk[0:nrow, :]),
                      reads=(bkey,), writes=(("B:stage", id(stage)),))
            P.dma("sp", dst[:, :], stage[0:nrow, 0:nchunk * 128], "ost", reads=(("B:stage", id(stage)),))

        def mixer_ab(tl):
            l = 0
            T = tl["T"]
            s = tl["s"]
            TB = min(T, 128)
            NB = T // TB
            ci = cidx(tl, l, 1)
            norm_mod(tl, l, 1)
            P.fence_big()
            dw = pkv("dw").rearrange("p (c k) -> p c k", c=8)
            for half in range(2):
                bav, bak = wget(win_v[:, :, (4 + half) * 512:(5 + half) * 512], [128, 16, 512], BF16)
                bgv, bgk = wget(win_v[:, :, (6 + half) * 512:(7 + half) * 512], [128, 16, 512], BF16)
                for c4 in range(4):
                    cc = half * 4 + c4
                    pa, pak = PS.next()
                    pg, pgk = PS.next()
                    P.pe(lambda t, pa=pa, bav=bav, c4=c4: [
                        mm(t, pa[:, :T], bav[:, kc, c4 * 128:(c4 + 1) * 128], xn[:, kc, :T], kc == 0, kc == KC - 1)
                        for kc in range(KC)][-1], reads=[bak] + XNK, writes=(pak,))
                    P.pe(lambda t, pg=pg, bgv=bgv, c4=c4: [
                        mm(t, pg[:, :T], bgv[:, kc, c4 * 128:(c4 + 1) * 128], xn[:, kc, :T], kc == 0, kc == KC - 1)
                        for kc in range(KC)][-1], reads=[bgk] + XNK, writes=(pgk,))
                    ts, tsk = TS.next()
                    P.act(lambda a, ts=ts, pg=pg: a.activation(out=ts[:, :T], in_=pg[:, :T], func=AF.Sigmoid),
                          reads=(pgk,), writes=(tsk,))
                    gl, glk = glu.next()
                    P.act(lambda a, gl=gl, cc=cc: a.copy(out=gl[:, 0:30], in_=convhist[s][:, cc, :]),
                          reads=(("ch", s),), writes=(glk,))
                    P.dve(lambda v, gl=gl, pa=pa, ts=ts: v.tensor_tensor(out=gl[:, 30:30 + T], in0=pa[:, :T], in1=ts[:, :T],
                                                                       op=ALU.mult),
                          reads=(pak, tsk, glk), writes=(glk,))
                    P.act(lambda a, gl=gl, cc=cc: a.copy(out=convhist[s][:, cc, :], in_=gl[:, T:T + 30]),
                          reads=(glk, ("ch", s)), writes=(("ch", s),))
                    for k in range(31):
                        if k == 0:
                            P.dve(lambda v, gl=gl, cc=cc: v.tensor_scalar(
                                out=acc[:, cc, :T], in0=gl[:, 0:T], scalar1=dw[:, cc, 0:1], scalar2=None, op0=ALU.mult),
                                reads=(glk, ("pk",)), writes=(("B:acc", cc),))
                        else:
                            P.dve(lambda v, gl=gl, cc=cc, k=k: v.scalar_tensor_tensor(
                                out=acc[:, cc, :T], in0=gl[:, k:k + T], scalar=dw[:, cc, k:k + 1], in1=acc[:, cc, :T],
                                op0=ALU.mult, op1=ALU.add), reads=(glk, ("B:acc", cc)), writes=(("B:acc", cc),))
            pm, pmk = PS.next()
            pq, pqk = PS.next()
            for cc in range(8):
                ts, tsk = TS.next()
                P.act(lambda a, ts=ts, cc=cc: a.activation(out=ts[:, :T], in_=acc[:, cc, :T], func=AF.Square),
                      reads=(("B:acc", cc),), writes=(tsk,))
                P.pe(lambda t, cc=cc: mm(t, pm[:, :T], ones_f, acc[:, cc, :T], cc == 0, cc == 7),
                     reads=(("B:acc", cc), ("ones",)), writes=(pmk,))
                P.pe(lambda t, cc=cc, ts=ts: mm(t, pq[:, :T], ones_f, ts[:, :T], cc == 0, cc == 7),
                     reads=(tsk, ("ones",)), writes=(pqk,))
            mean, mk = glu.aps[0][:, 0:512], ("B:glu", 0)
            lrs, lk = glu.aps[1][:, 0:512], ("B:glu", 1)
            P.dve(lambda v: v.tensor_scalar(out=mean[:, :T], in0=pm[:, :T], scalar1=1.0 / 1024, scalar2=None, op0=ALU.mult),
                  reads=(pmk,), writes=(mk,))
            P.dve(lambda v: v.tensor_tensor(out=lrs[:, :T], in0=mean[:, :T], in1=mean[:, :T], op=ALU.mult),
                  reads=(mk,), writes=(lk,))
            P.dve(lambda v: v.scalar_tensor_tensor(out=lrs[:, :T], in0=pq[:, :T], scalar=1.0 / 1024, in1=lrs[:, :T],
                                                   op0=ALU.mult, op1=ALU.subtract), reads=(pqk, lk), writes=(lk,))
            P.act(lambda a: a.activation(out=lrs[:, :T], in_=lrs[:, :T], func=AF.Sqrt, bias=eps_t[:, 0:1], scale=1.0),
                  reads=(lk, ("eps",)), writes=(lk,))
            P.dve(lambda v: v.reciprocal(out=lrs[:, :T], in_=lrs[:, :T]), reads=(lk,), writes=(lk,))
            for cc in range(8):
                tm, tk = TM.next()
                P.dve(lambda v, tm=tm, cc=cc: v.tensor_tensor(out=tm[:, :T], in0=acc[:, cc, :T], in1=mean[:, :T],
                                                             op=ALU.subtract), reads=(("B:acc", cc), mk), writes=(tk,))
                P.dve(lambda v, tm=tm: v.tensor_tensor(out=tm[:, :T], in0=tm[:, :T], in1=lrs[:, :T], op=ALU.mult),
                      reads=(tk, lk), writes=(tk,))
                P.act(lambda a, tm=tm, cc=cc: a.activation(out=ab[:, 8 + cc, :T], in_=tm[:, :T], func=AF.Silu,
                                                          scale=pkv("ln_g")[:, cc:cc + 1], bias=pkv("ln_b")[:, cc:cc + 1]),
                      reads=(tk, ("pk",)), writes=(("B:ab", 8 + cc),))
            v0, v0k = wget(win_v[:, :, 2 * 512:3 * 512], [128, 16, 512], BF16)
            v1, v1k = wget(win_v[:, :, 3 * 512:4 * 512], [128, 16, 512], BF16)
            for tb in range(NB):
                vgt, vgk = vg.next()
                for hv, (vv, vk) in enumerate(((v0, v0k), (v1, v1k))):
                    pv, pvk = PS.next()
                    P.pe(lambda t, pv=pv, vv=vv, tb=tb: [
                        mm(t, pv[0:TB, :], xn[:, kc, tb * TB:(tb + 1) * TB], vv[:, kc, :], kc == 0, kc == KC - 1)
                        for kc in range(KC)][-1], reads=[vk] + XNK, writes=(pvk,))
                    P.act(lambda a, pv=pv, vgt=vgt, hv=hv: a.activation(out=vgt[0:TB, hv * 512:(hv + 1) * 512],
                                                                     in_=pv[0:TB, :], func=AF.Gelu_apprx_tanh),
                          reads=(pvk,), writes=(vgk,))
                st, stk = TM.next()
                ts, tsk = TS.next()
                P.dve(lambda v, vgt=vgt, st=st: v.tensor_reduce(out=st[0:TB, 0:1], in_=vgt[0:TB, :], axis=AX.X, op=ALU.add),
                      reads=(vgk,), writes=(stk,))
                P.act(lambda a, vgt=vgt, ts=ts: a.activation(out=ts[0:TB, :], in_=vgt[0:TB, 0:512], func=AF.Square),
                      reads=(vgk,), writes=(tsk,))
                P.dve(lambda v, ts=ts, st=st: v.tensor_reduce(out=st[0:TB, 1:2], in_=ts[0:TB, :], axis=AX.X, op=ALU.add),
                      reads=(tsk, stk), writes=(stk,))
                P.act(lambda a, vgt=vgt, ts=ts: a.activation(out=ts[0:TB, :], in_=vgt[0:TB, 512:1024], func=AF.Square),
                      reads=(vgk, tsk), writes=(tsk,))
                P.dve(lambda v, ts=ts, st=st: v.tensor_reduce(out=st[0:TB, 4:5], in_=ts[0:TB, :], axis=AX.X, op=ALU.add),
                      reads=(tsk, stk), writes=(stk,))
                P.dve(lambda v, st=st: v.tensor_tensor(out=st[0:TB, 1:2], in0=st[0:TB, 1:2], in1=st[0:TB, 4:5], op=ALU.add),
                      reads=(stk,), writes=(stk,))
                P.dve(lambda v, st=st: v.tensor_scalar(out=st[0:TB, 2:3], in0=st[0:TB, 0:1], scalar1=1.0 / 1024, scalar2=None,
                                                       op0=ALU.mult), reads=(stk,), writes=(stk,))
                P.dve(lambda v, st=st: v.tensor_tensor(out=st[0:TB, 3:4], in0=st[0:TB, 2:3], in1=st[0:TB, 2:3], op=ALU.mult),
                      reads=(stk,), writes=(stk,))
                P.dve(lambda v, st=st: v.scalar_tensor_tensor(out=st[0:TB, 3:4], in0=st[0:TB, 1:2], scalar=1.0 / 1024,
                                                              in1=st[0:TB, 3:4], op0=ALU.mult, op1=ALU.subtract),
                      reads=(stk,), writes=(stk,))
                P.act(lambda a, st=st: a.activation(out=st[0:TB, 3:4], in_=st[0:TB, 3:4], func=AF.Sqrt, bias=eps_t[0:TB, 0:1],
                                                    scale=1.0), reads=(stk, ("eps",)), writes=(stk,))
                P.dve(lambda v, st=st: v.reciprocal(out=st[0:TB, 3:4], in_=st[0:TB, 3:4]), reads=(stk,), writes=(stk,))
                P.dve(lambda v, vgt=vgt, st=st: v.tensor_scalar(out=vgt[0:TB, :], in0=vgt[0:TB, :], scalar1=st[0:TB, 2:3],
                                                                scalar2=st[0:TB, 3:4], op0=ALU.subtract, op1=ALU.mult),
                      reads=(vgk, stk), writes=(vgk,))
                P.dve(lambda v, vgt=vgt: v.tensor_tensor(out=vgt[0:TB, :], in0=vgt[0:TB, :], in1=vgb[0:TB, 0:1024], op=ALU.mult),
                      reads=(vgk, ("vgb",)), writes=(vgk,))
                P.dve(lambda v, vgt=vgt: v.tensor_tensor(out=vgt[0:TB, :], in0=vgt[0:TB, :], in1=vgb[0:TB, 1024:2048], op=ALU.add),
                      reads=(vgk, ("vgb",)), writes=(vgk,))
                if tl["kind"] == "sample":
                    P.dma("sp", nav[:, :], vgt[0:TB, :], "ost", reads=(vgk,))
                P.act(lambda a, vgt=vgt, tb=tb: a.copy(out=vnb[0:TB, tb, :], in_=vgt[0:TB, :]), reads=(vgk,),
                      writes=(("B:vnb", tb),))
            for half in range(2):
                uv, uk = wget(win_v[:, :, half * 512:(half + 1) * 512], [128, 16, 512], BF16)
                for c4 in range(4):
                    cc = half * 4 + c4
                    g = cc // 2
                    pu_, puk = PS.next()
                    P.pe(lambda t, pu_=pu_, uv=uv, c4=c4: [
                        mm(t, pu_[:, :T], uv[:, kc, c4 * 128:(c4 + 1) * 128], xn[:, kc, :T], kc == 0, kc == KC - 1)
                        for kc in range(KC)][-1], reads=[uk] + XNK, writes=(puk,))
                    ts, tsk = TS.next()
                    P.act(lambda a, ts=ts, pu_=pu_: a.activation(out=ts[:, :T], in_=pu_[:, :T], func=AF.Gelu_apprx_tanh),
                          reads=(puk,), writes=(tsk,))
                    pgt, pgtk = PS.next()

                    def fn(t, pgt=pgt, cc=cc, g=g):
                        ins = None
                        for tb in range(NB):
                            o = pgt[:, tb * TB:(tb + 1) * TB]
                            mm(t, o, vnb[0:TB, tb, cc * 128:(cc + 1) * 128], WmT[0:TB, g, 0:TB], True, False)
                            ins = mm(t, o, ones_f[0:1, :], abs_t[0:1, g * 128:g * 128 + TB], False, True)
                        return ins
                    P.pe(fn, reads=[("B:vnb", tb) for tb in range(NB)] + [("WmT",), ("abs",), ("ones",)], writes=(pgtk,))
                    P.dve(lambda v, ts=ts, pgt=pgt, cc=cc: v.tensor_tensor(out=ab[:, cc, :T], in0=pgt[:, :T], in1=ts[:, :T],
                                                                         op=ALU.mult),
                          reads=(pgtk, tsk), writes=(("B:ab", cc),))
            ABK = [("B:ab", k) for k in range(16)]
            for dp in range(8):
                ov, ok = wget(wout_v[:, :, dp * 256:(dp + 1) * 256], [128, 16, 256], BF16)
                b0, b0k = PS.next()
                b1, b1k = PS.next()

                def fn(t, ov=ov, b0=b0, b1=b1):
                    ins = None
                    for kc in range(16):
                        for dci, bb in enumerate((b0, b1)):
                            ins = mm(t, bb[:, :T], ov[:, kc, dci * 128:(dci + 1) * 128], ab[:, kc, :T], kc == 0, kc == 15)
                    return ins
                P.pe(fn, reads=[ok] + ABK, writes=(b0k, b1k))
                for dci, (bb, bbk) in enumerate(((b0, b0k), (b1, b1k))):
                    dc = 2 * dp + dci
                    P.dve(lambda v, bb=bb, dc=dc: v.scalar_tensor_tensor(
                        out=h[:, dc, :T], in0=bb[:, :T], scalar=Gall[:, ci, dc:dc + 1], in1=h[:, dc, :T],
                        op0=ALU.mult, op1=ALU.add), reads=(bbk, ("h", dc), ("consts",)), writes=(("h", dc),))
                    ssq_update(tl, dc)

        def mixer_c(tl):
            l = 1
            T = tl["T"]
            s = tl["s"]
            L = 15 + T
            ci = cidx(tl, l, 1)
            stats(tl)
            P.fence_big()
            for g in range(4):
                w = 2 ** (g + 1)
                hm, hmk = HMG.next()
                P.act(lambda a, hm=hm, g=g: a.copy(out=hm[:, :, 0:15], in_=poolhist[s][:, 4 * g:4 * g + 4, :]),
                      reads=(("ph", s, g),), writes=(hmk,))
                for j in range(4):
                    dc = 4 * g + j
                    tm, tk = TM.next()
                    P.dve(lambda v, dc=dc, tm=tm: v.scalar_tensor_tensor(
                        out=tm[:, :T], in0=h[:, dc, :T], scalar=Aall[:, ci, dc:dc + 1], in1=rstd[:, :T],
                        op0=ALU.mult, op1=ALU.mult), reads=(("h", dc), ("rstd",), ("consts",)), writes=(tk,))
                    P.act(lambda a, dc=dc, tm=tm, hm=hm, j=j: a.activation(
                        out=hm[:, j, 15:15 + T], in_=tm[:, :T], func=AF.Identity,
                        bias=mod[l][:, 3 * 16 + dc, s:s + 1], scale=1.0), reads=(tk, ("mod", l), hmk), writes=(hmk,))
                P.act(lambda a, hm=hm, g=g: a.copy(out=poolhist[s][:, 4 * g:4 * g + 4, :], in_=hm[:, :, T:T + 15]),
                      reads=(hmk,), writes=(("ph", s, g),))
                if tl["kind"] == "halo":
                    continue
                srcs = [hm, sA, sB, sA, sB]
                skeys = [hmk, ("B:sA",), ("B:sB",), ("B:sA",), ("B:sB",)]
                for lev in range(1, g + 2):
                    sh = 2 ** (lev - 1)
                    lo = 2 ** lev - 1
                    P.dve(lambda v, a=srcs[lev - 1], o=srcs[lev], sh=sh, lo=lo: v.tensor_tensor(
                        out=o[:, :, lo:L], in0=a[:, :, lo:L], in1=a[:, :, lo - sh:L - sh], op=ALU.add),
                        reads=(skeys[lev - 1],), writes=(skeys[lev],))
                sres, sresk = srcs[g + 1], skeys[g + 1]
                pl, plk = PL.next()
                P.dve(lambda v, sres=sres, hm=hm, pl=pl, w=w: v.scalar_tensor_tensor(
                    out=pl[:, :, :T], in0=sres[:, :, 15:L], scalar=1.0 / w, in1=hm[:, :, 15:L], op0=ALU.mult,
                    op1=ALU.subtract), reads=(sresk, hmk), writes=(plk,))
                if tl["first"]:
                    iv = pkv("invc").rearrange("p (g j t) -> p g j t", g=4, j=4)[:, g, :, :]
                    tm, tk = TM.next()
                    tmv = tm[:, 0:64].rearrange("p (j t) -> p j t", j=4)
                    P.dve(lambda v, sres=sres, tmv=tmv, iv=iv: v.tensor_tensor(out=tmv, in0=sres[:, :, 15:31], in1=iv,
                                                                             op=ALU.mult),
                          reads=(sresk, ("pk",)), writes=(tk,))
                    P.dve(lambda v, tmv=tmv, hm=hm, pl=pl: v.tensor_tensor(out=pl[:, :, 0:16], in0=tmv, in1=hm[:, :, 15:31],
                                                                         op=ALU.subtract),
                          reads=(tk, hmk, plk), writes=(plk,))
                cv, ck = wget(cw_v[g], [128, 4, 512], BF16)
                for jo in range(4):
                    dc = 4 * g + jo
                    bk, bkey = PS.next()
                    P.pe(lambda t, bk=bk, cv=cv, jo=jo, pl=pl: [
                        mm(t, bk[:, :T], cv[:, kc, jo * 128:(jo + 1) * 128], pl[:, kc, :T], kc == 0, kc == 3)
                        for kc in range(4)][-1], reads=(ck, plk), writes=(bkey,))
                    tm, tk = TM.next()
                    P.act(lambda a, bk=bk, tm=tm, dc=dc: a.activation(
                        out=tm[:, :T], in_=bk[:, :T], func=AF.Identity, scale=sgall[:, s, dc:dc + 1],
                        bias=bsgall[:, s, dc:dc + 1]), reads=(bkey, ("consts",)), writes=(tk,))
                    P.dve(lambda v, tm=tm, dc=dc: v.tensor_tensor(out=h[:, dc, :T], in0=h[:, dc, :T], in1=tm[:, :T], op=ALU.add),
                          reads=(tk, ("h", dc)), writes=(("h", dc),))
                    ssq_update(tl, dc)

        def final(tl):
            T = tl["T"]
            TB = min(T, 128)
            stats(tl)
            P.fence_big()
            fg = pkv("final_g")
            for dc in range(KC):
                P.dve(lambda v, dc=dc: v.scalar_tensor_tensor(
                    out=yn[:, dc, :T], in0=h[:, dc, :T], scalar=fg[:, dc:dc + 1], in1=rstd[:, :T], op0=ALU.mult,
                    op1=ALU.mult), reads=(("h", dc), ("rstd",), ("pk",)), writes=(("B:yn", dc),))
            for tb in range(T // TB):
                for q in range(2):
                    ysb, ysk = YS.next()
                    for hb in range(2):
                        bk, bkey = PS.next()
                        P.pe(lambda t, bk=bk, q=q, hb=hb, tb=tb: [
                            t.transpose(bk[0:TB, j * 128:(j + 1) * 128], yn[:, 8 * q + 4 * hb + j, tb * TB:(tb + 1) * TB], ident)
                            for j in range(4)][-1],
                            reads=[("B:yn", 8 * q + 4 * hb + j) for j in range(4)] + [("ident",)], writes=(bkey,))
                        P.act(lambda a, bk=bk, ysb=ysb, hb=hb: a.copy(out=ysb[0:TB, hb * 512:(hb + 1) * 512], in_=bk[0:TB, :]),
                              reads=(bkey,), writes=(ysk,))
                    P.dma("sp", tl["y"][tb * TB:(tb + 1) * TB, q * 1024:(q + 1) * 1024], ysb[0:TB, :], f"y{ysk[1]}",
                          reads=(ysk,))

        def run_tile(tl):
            load_x(tl)
            ffn(tl, 0, 0)
            mixer_ab(tl)
            ffn(tl, 0, 1)
            ffn(tl, 1, 0)
            mixer_c(tl)
            if tl["kind"] == "halo":
                return
            ffn(tl, 1, 1)
            final(tl)
            if tl["last"]:
                s = tl["s"]
                out_state(convhist[s], 8, 30, ncst, ncp if s == 0 else ncs, [("ch", s)])
                out_state(poolhist[s], 16, 15, npst, npp if s == 0 else nps, [("ph", s, g) for g in range(4)])

        import os as _os
        _stop = _os.environ.get("MK_DEBUG_STOP", "")
        if _stop != "prologue":
          run_tile(dict(kind="halo", T=128, s=0, x=xh, y=None, first=False, last=False, ssq_n=0))
        flag = pkv("flag")
        P.dve(lambda v: v.tensor_scalar(out=convhist[0], in0=convhist[0], scalar1=flag[:, 0:1], scalar2=None, op0=ALU.mult),
              reads=(("ch", 0), ("pk",)), writes=(("ch", 0),))
        P.dve(lambda v: v.tensor_scalar(out=poolhist[0], in0=poolhist[0], scalar1=flag[:, 0:1], scalar2=None, op0=ALU.mult),
              reads=[("ph", 0, g) for g in range(4)] + [("pk",)], writes=tuple(("ph", 0, g) for g in range(4)))
        if _stop not in ("prologue", "halo"):
          run_tile(dict(kind="sample", T=64, s=1, x=xsm, y=ysm, first=False, last=True, ssq_n=0))
        for it in range(NT if _stop == "" else 0):
            run_tile(dict(kind="prompt", T=512, s=0, x=xp[it * 512:(it + 1) * 512, :], y=yp[it * 512:(it + 1) * 512, :],
                          first=(it == 0), last=(it == NT - 1), ssq_n=0))

        sem_names = [s for s in P.cnt]
        sems = {}
        for sname in sem_names:
            sems[sname] = es.enter_context(nc.semaphore(f"sem_{sname}"))
        block = es.enter_context(nc.Block())
        P.emit(nc, block, sems)
    return nc


def _fm(v, chunks):
    return np.ascontiguousarray(np.asarray(v, np.float32).reshape(chunks, 128).T)


def make_in_maps(inp, NT):
    SEQH = NT * 512
    f32 = np.float32
    import os as _os
    if _os.environ.get("MK_TINYW", "") == "1":
        shared = dict(
            wsm=np.ascontiguousarray(inp["ffn_w_gate"][0, 0], f32),
            aws=np.ascontiguousarray(inp["a_ws"][0], f32),
            abs=np.ascontiguousarray(np.asarray(inp["a_bs"][0], f32).reshape(1, 512)),
            rows=np.ascontiguousarray(np.concatenate([inp["a_v_norm_g"][0], inp["a_v_norm_b"][0]]).reshape(1, 2048), f32),
        )
    else:
      shared = dict(
        ada_w=np.ascontiguousarray(inp["ada_w"], f32),
        wg=np.ascontiguousarray(inp["ffn_w_gate"], f32), wu=np.ascontiguousarray(inp["ffn_w_up"], f32),
        wd=np.ascontiguousarray(inp["ffn_w_down"], f32), win=np.ascontiguousarray(inp["ab_w_in"], f32),
        wout=np.ascontiguousarray(inp["ab_w_out"], f32), cw=np.ascontiguousarray(inp["c_w_grp"], f32),
        aws=np.ascontiguousarray(inp["a_ws"][0], f32),
        abs=np.ascontiguousarray(np.asarray(inp["a_bs"][0], f32).reshape(1, 512)),
        rows=np.ascontiguousarray(np.concatenate([inp["a_v_norm_g"][0], inp["a_v_norm_b"][0]]).reshape(1, 2048), f32),
    )
    pk_sh = np.zeros((128, NPK), f32)

    def put(pkarr, name, val):
        o, w = PK[name]
        pkarr[:, o:o + w] = np.asarray(val, f32).reshape(128, w)

    put(pk_sh, "ada_b", np.concatenate([_fm(inp["ada_b"][l], 144) for l in range(2)], axis=1))
    put(pk_sh, "norm_g", np.concatenate([_fm(inp["norm_g"][l, i], 16) for l in range(2) for i in range(3)], axis=1))
    put(pk_sh, "final_g", _fm(inp["final_norm_g"], 16))
    put(pk_sh, "dw", np.asarray(inp["b_dw"][0], f32).T.reshape(8, 128, 31).transpose(1, 0, 2))
    put(pk_sh, "ln_g", _fm(inp["b_ln_g"][0], 8))
    put(pk_sh, "ln_b", _fm(inp["b_ln_b"][0], 8))
    put(pk_sh, "c_b", _fm(np.asarray(inp["c_b_grp"][0]).reshape(-1), 16))
    put(pk_sh, "c_scale", _fm(inp["c_scale"][0], 16))
    maps = []
    for c in range(NCORES):
        bp, half, bs = c // 2, c % 2, c
        m = dict(shared)
        m["xp"] = np.ascontiguousarray(inp["x_prompt"][bp, half * SEQH:(half + 1) * SEQH], f32)
        m["xh"] = (np.ascontiguousarray(inp["x_prompt"][bp, SEQH - 128:SEQH], f32) if half == 1
                   else np.zeros((128, D), f32))
        m["xsm"] = np.ascontiguousarray(inp["x_sample"][bs], f32)
        pkc = pk_sh.copy()
        c2 = np.stack([np.asarray(inp["c_prompt"][bp], f32), np.asarray(inp["c_sample"][bs], f32)], axis=-1)
        put(pkc, "c2T", c2.reshape(16, 128, 2).transpose(1, 0, 2))
        put(pkc, "sconv", np.asarray(inp["state_conv_b"][0, bs], f32).T.reshape(8, 128, 30).transpose(1, 0, 2))
        put(pkc, "spool", np.asarray(inp["state_pool_c"][0, bs], f32).T.reshape(16, 128, 15).transpose(1, 0, 2))
        put(pkc, "flag", np.full((128, 1), float(half), f32))
        pos = half * SEQH + np.arange(16)
        iv = np.stack([1.0 / np.minimum(2 ** (g + 1), pos + 1) for g in range(4)]).astype(f32)
        put(pkc, "invc", np.broadcast_to(iv[None, :, None, :], (128, 4, 4, 16)))
        m["pk"] = pkc
        maps.append(m)
    return maps


_NC_CACHE = {}


def run(inp, NT):
    if NT not in _NC_CACHE:
        _NC_CACHE[NT] = build(NT)
    nc = _NC_CACHE[NT]
    maps = make_in_maps(inp, NT)
    res = run_bass_kernel_spmd(nc, maps, core_ids=list(range(NCORES)))
    R = res.results
    SEQH = NT * 512
    y_prompt = np.stack([np.concatenate([R[2 * b]["yp"], R[2 * b + 1]["yp"]], axis=0) for b in range(4)])
    y_sample = np.stack([R[c]["ysm"] for c in range(8)])
    ncp = np.stack([R[2 * b + 1]["ncp"] for b in range(4)])[None]
    ncs = np.stack([R[c]["ncs"] for c in range(8)])[None]
    npp = np.stack([R[2 * b + 1]["npp"] for b in range(4)])[None]
    nps = np.stack([R[c]["nps"] for c in range(8)])[None]
    nav = np.stack([R[c]["nav"] for c in range(8)])[None]
    return tuple(np.ascontiguousarray(a, dtype=np.float32) for a in (y_prompt, y_sample, ncp, ncs, npp, nps, nav))


def kernel(**inputs):
    inp = {k: np.asarray(v) for k, v in inputs.items()}
    return run(inp, 8)
```

```python
import numpy as np
from contextlib import ExitStack
import concourse.bass as bass
import concourse.mybir as mybir
from concourse.bass_utils import run_bass_kernel_spmd

F32 = mybir.dt.float32
BF16 = mybir.dt.bfloat16
AF = mybir.ActivationFunctionType
ALU = mybir.AluOpType
AX = mybir.AxisListType

D = 2048
DFF = 5632
KC = 16
MC = 44
EPS = 1e-6
NCORES = 8
SLOT_W = 4096
NS = 4
BIGW = 13376

PK = {}
_o = 0
for _n, _w in [("ada_b", 288), ("norm_g", 96), ("final_g", 16), ("dw", 248), ("ln_g", 8), ("ln_b", 8),
               ("c_b", 16), ("c_scale", 16), ("c2T", 32), ("sconv", 240), ("spool", 240), ("flag", 1),
               ("invc", 256)]:
    PK[_n] = (_o, _w)
    _o += _w
NPK = _o


class Prog:
    ENGS = ("pe", "act", "dve", "pool", "sp")

    def __init__(self):
        self.q = {e: [] for e in self.ENGS}
        self.cnt = {}
        self.keys = {}
        self.big_base = {}

    def _entry(self, k):
        e = self.keys.get(k)
        if e is None:
            base = dict(self.big_base) if (isinstance(k[0], str) and k[0].startswith("B:")) else {}
            e = [None, base]
            self.keys[k] = e
        return e

    def fence_big(self):
        merged = dict(self.big_base)
        for k in [k for k in self.keys if isinstance(k[0], str) and k[0].startswith("B:")]:
            w, rd = self.keys.pop(k)
            if w is not None:
                merged[w[0]] = max(merged.get(w[0], 0), w[1])
            for s, v in rd.items():
                merged[s] = max(merged.get(s, 0), v)
        self.big_base = merged

    def _record(self, eng, fn, reads, writes, sem, inc):
        waits = {}

        def need(tok):
            if tok is not None:
                waits[tok[0]] = max(waits.get(tok[0], 0), tok[1])

        for k in reads:
            need(self._entry(k)[0])
        for k in writes:
            e = self._entry(k)
            need(e[0])
            for s, v in e[1].items():
                need((s, v))
        if eng == "pe":
            waits.pop("pe", None)
        self.cnt[sem] = self.cnt.get(sem, 0) + inc
        tok = (sem, self.cnt[sem])
        self.q[eng].append((fn, waits, (sem, inc)))
        for k in reads:
            e = self._entry(k)
            e[1][sem] = max(e[1].get(sem, 0), tok[1])
        for k in writes:
            self.keys[k] = [tok, {}]
        return tok

    def op(self, eng, fn, reads=(), writes=()):
        return self._record(eng, fn, reads, writes, eng, 1)

    def pe(self, fn, reads=(), writes=()):
        return self.op("pe", fn, reads, writes)

    def act(self, fn, reads=(), writes=()):
        return self.op("act", fn, reads, writes)

    def dve(self, fn, reads=(), writes=()):
        return self.op("dve", fn, reads, writes)

    def pool(self, fn, reads=(), writes=()):
        return self.op("pool", fn, reads, writes)

    def dma(self, queue, out, in_, sem, reads=(), writes=()):
        return self._record(queue, lambda e: e.dma_start(out=out, in_=in_), reads, writes, sem, 16)

    def emit(self, nc, block, sems):
        prog = self

        def run(eng_name, eng):
            waited = {}
            for fn, waits, (sem, inc) in prog.q[eng_name]:
                for s in sorted(waits):
                    v = waits[s]
                    if waited.get(s, 0) < v:
                        eng.wait_ge(sems[s], v)
                        waited[s] = v
                ins = fn(eng)
                ins.then_inc(sems[sem], inc)
            if eng_name == "sp":
                for s, v in prog.cnt.items():
                    if s not in prog.ENGS and waited.get(s, 0) < v:
                        eng.wait_ge(sems[s], v)

        @block.tensor
        def _(t):
            run("pe", t)

        @block.scalar
        def _(a):
            run("act", a)

        @block.vector
        def _(v):
            run("dve", v)

        @block.gpsimd
        def _(g):
            run("pool", g)

        @block.sync
        def _(s):
            run("sp", s)


class Rot:
    def __init__(self, name, aps):
        self.aps = aps
        self.name = name
        self.i = 0

    def next(self):
        i = self.i % len(self.aps)
        self.i += 1
        return self.aps[i], (self.name, i)


def build(NT):
    TOK = NT * 512
    nc = bass.Bass("TRN2", target_bir_lowering=False)
    P = Prog()

    def din(name, shape):
        return nc.dram_tensor(name, list(shape), F32, kind="ExternalInput").ap()

    def dout(name, shape):
        return nc.dram_tensor(name, list(shape), F32, kind="ExternalOutput").ap()

    xp = din("xp", [TOK, D]); xh = din("xh", [128, D]); xsm = din("xsm", [64, D])
    pk_d = din("pk", [128, NPK]); rows_d = din("rows", [1, 2048]); abs_d = din("abs", [1, 512])
    aws_d = din("aws", [4, 128, 128])
    import os as _os
    TINYW = _os.environ.get("MK_TINYW", "") == "1"
    if not TINYW:
        ada_w = din("ada_w", [2, D, 9 * D])
        wg = din("wg", [2, 2, D, DFF]); wu = din("wu", [2, 2, D, DFF]); wd = din("wd", [2, 2, DFF, D])
        win = din("win", [1, D, 4096]); wout = din("wout", [1, D, D]); cw = din("cw", [1, 4, 512, 512])
    else:
        wsm = din("wsm", [D, DFF])
    yp = dout("yp", [TOK, D]); ysm = dout("ysm", [64, D])
    ncp = dout("ncp", [30, 1024]); ncs = dout("ncs", [30, 1024])
    npp = dout("npp", [15, D]); nps = dout("nps", [15, D]); nav = dout("nav", [64, 1024])

    with ExitStack() as es:
        def sb(name, shape, dt=F32):
            return es.enter_context(nc.sbuf_tensor(name, list(shape), dt))

        h_t = sb("h", [128, KC, 512]); h = h_t[:]
        xn_t = sb("xn", [128, KC, 512], BF16); xn = xn_t[:]
        big_t = sb("big", [128, BIGW])
        slots = [sb(f"slot{i}", [128, SLOT_W]) for i in range(NS)]
        xs = Rot("xs", [sb(f"xs{i}", [128, 512])[:] for i in range(2)])
        TS = Rot("ts", [sb(f"ts{i}", [128, 512])[:] for i in range(2)])
        TM = Rot("tm", [sb(f"tmm{i}", [128, 512])[:] for i in range(2)])
        sqs = sb("sqs", [128, 512])[:]
        rstd = sb("rstd", [128, 512])[:]
        pk = sb("pkt", [128, NPK])[:]
        vgb = sb("vgb", [128, 2048])[:]
        abs_t = sb("abst", [1, 512])[:]
        ident = sb("ident", [128, 128])[:]
        ones_f = sb("onesf", [128, 128])[:]
        eps_t = sb("epst", [128, 1])[:]
        WmT = sb("WmT", [128, 4, 128], BF16)[:]
        sc = sb("sc", [128, 32], BF16)[:]
        ssq = sb("ssq", [128, 512])[:]
        mod = [sb(f"mod{l}", [128, 144, 2])[:] for l in range(2)]
        Aall = sb("Aall", [128, 12, 16])[:]
        Gall = sb("Gall", [128, 12, 16])[:]
        sgall = sb("sgall", [128, 2, 16])[:]
        bsgall = sb("bsgall", [128, 2, 16])[:]
        convhist = [sb(f"convhist{s}", [128, 8, 30])[:] for s in range(2)]
        poolhist = [sb(f"poolhist{s}", [128, 16, 15])[:] for s in range(2)]
        banks = [es.enter_context(nc.psum_tensor(f"bank{i}", [128, 512], F32))[:] for i in range(8)]
        PS = Rot("ps", banks)

        big = big_t[:]

        def bview(off, words, dt=F32, pattern=None, **kw):
            v = big[:, off:off + words]
            if dt == BF16:
                v = v.bitcast(BF16)
            if pattern:
                v = v.rearrange(pattern, **kw)
            return v

        act = bview(0, 11264, BF16, "p (m t) -> p m t", m=MC)
        sq = bview(0, 8192, F32, "p (k t) -> p k t", k=KC)
        vg = Rot("B:vg", [bview(0, 1024), bview(1024, 1024)])
        vnb = bview(2048, 2048, BF16, "p (b c) -> p b c", b=4)
        glu = Rot("B:glu", [bview(4096, 544), bview(4640, 544)])
        acc = bview(5184, 4096, F32, "p (c t) -> p c t", c=8)
        ab = bview(9280, 4096, BF16, "p (k t) -> p k t", k=16)
        HMG = Rot("B:hmg", [bview(0, 2112, F32, "p (j t) -> p j t", j=4), bview(2112, 2112, F32, "p (j t) -> p j t", j=4)])
        sA = bview(4224, 2112, F32, "p (j t) -> p j t", j=4)
        sB = bview(6336, 2112, F32, "p (j t) -> p j t", j=4)
        PL = Rot("B:pl", [bview(8448, 1024, BF16, "p (j t) -> p j t", j=4), bview(9472, 1024, BF16, "p (j t) -> p j t", j=4)])
        yn = bview(0, 8192, F32, "p (k t) -> p k t", k=KC)
        YS = Rot("B:ys", [bview(8192, 1024), bview(9216, 1024)])
        ncst = bview(10240, 1024)
        npst = bview(11264, 2048)
        aws_st = bview(0, 512, F32, "p (g j) -> p g j", g=4)

        def pkv(name, a=0, b=None):
            o, w = PK[name]
            b = w if b is None else b
            return pk[:, o + a:o + b]

        ring_i = [0]

        def wget(src, shape, dt):
            s = ring_i[0] % NS
            ring_i[0] += 1
            words = int(np.prod(shape[1:])) // (2 if dt == BF16 else 1)
            v = slots[s][:, 0:words]
            if dt == BF16:
                v = v.bitcast(BF16)
            if len(shape) == 3:
                v = v.rearrange("p (a b) -> p a b", a=shape[1])
            key = ("slot", s)
            P.dma("pool", v, src, f"w{s}", reads=(), writes=(key,))
            return v, key

        if not TINYW:
            wg_v = [[wg[l, f].rearrange("(kc p) n -> p kc n", p=128) for f in range(2)] for l in range(2)]
            wu_v = [[wu[l, f].rearrange("(kc p) n -> p kc n", p=128) for f in range(2)] for l in range(2)]
            wd_v = [[wd[l, f].rearrange("(m p) n -> p m n", p=128) for f in range(2)] for l in range(2)]
            win_v = win[0].rearrange("(kc p) n -> p kc n", p=128)
            wout_v = wout[0].rearrange("(kc p) n -> p kc n", p=128)
            cw_v = [cw[0, g].rearrange("(kc p) n -> p kc n", p=128) for g in range(4)]
            ada_v = [ada_w[l].rearrange("(kc p) n -> p kc n", p=128) for l in range(2)]
        else:
            w_kc = wsm.rearrange("(kc p) n -> p kc n", p=128)
            wg_v = [[w_kc for f in range(2)] for l in range(2)]
            wu_v = wg_v
            w_d = wsm.rearrange("a b -> (a b)").rearrange("(r c) -> r c", c=D).rearrange("(m p) n -> p m n", p=128)
            wd_v = [[w_d for f in range(2)] for l in range(2)]
            win_v = w_kc
            wout_v = w_kc
            cw_v = [wsm[0:512, g * 512:(g + 1) * 512].rearrange("(kc p) n -> p kc n", p=128) for g in range(4)]

            class _AdaV:
                def __getitem__(self, idx):
                    sl = idx[2]
                    jb = (sl.start // 512) % 11
                    return w_kc[:, :, jb * 512:(jb + 1) * 512]
            ada_v = [_AdaV(), _AdaV()]

        def mm(t, out, lhsT, rhs, start, stop):
            return t.matmul(out, lhsT, rhs, start=start, stop=stop)

        P.dma("sp", pk, pk_d[:, :], "c0", writes=(("pk",),))
        P.dma("sp", vgb, rows_d[0:1, :].partition_broadcast(128), "c1", writes=(("vgb",),))
        P.dma("sp", abs_t, abs_d[:, :], "c2", writes=(("abs",),))
        P.fence_big()
        P.dma("sp", aws_st, aws_d.rearrange("g i j -> i g j"), "c3", writes=(("B:aws",),))
        P.pool(lambda g: g.memset(ident, 0.0), writes=(("ident",),))
        P.pool(lambda g: g.affine_select(out=ident, in_=ident, compare_op=ALU.not_equal, fill=1.0, base=0,
                                         pattern=[[-1, 128]], channel_multiplier=1),
               reads=(("ident",),), writes=(("ident",),))
        P.dve(lambda v: v.memset(ones_f, 1.0), writes=(("ones",),))
        P.dve(lambda v: v.memset(eps_t, EPS), writes=(("eps",),))
        P.dve(lambda v: v.memset(convhist[0], 0.0), writes=tuple(("ch", 0, cc) for cc in range(8)))
        P.dve(lambda v: v.memset(poolhist[0], 0.0), writes=tuple(("ph", 0, g) for g in range(4)))
        P.dve(lambda v: v.tensor_copy(out=convhist[1], in_=pkv("sconv").rearrange("p (c k) -> p c k", c=8)),
              reads=(("pk",),), writes=tuple(("ch", 1, cc) for cc in range(8)))
        P.dve(lambda v: v.tensor_copy(out=poolhist[1], in_=pkv("spool").rearrange("p (c k) -> p c k", c=16)),
              reads=(("pk",),), writes=tuple(("ph", 1, g) for g in range(4)))
        bk, bkey = PS.next()
        P.pe(lambda t: [t.transpose(bk[:, g * 128:(g + 1) * 128], aws_st[:, g, :], ident) for g in range(4)][-1],
             reads=(("B:aws",), ("ident",)), writes=(bkey,))
        P.act(lambda a: a.copy(out=WmT, in_=bk.rearrange("p (g i) -> p g i", g=4)), reads=(bkey,), writes=(("WmT",),))
        P.dve(lambda v: v.memset(WmT[64:128, :, 0:64], 0.0), reads=(("WmT",),), writes=(("WmT",),))
        P.act(lambda a: a.activation(out=sc, in_=pkv("c2T"), func=AF.Silu), reads=(("pk",),), writes=(("sc",),))
        for l in range(2):
            bk, bkey = PS.next()
            for jb in range(36):
                blk, bkk = wget(ada_v[l][:, :, jb * 512:(jb + 1) * 512], [128, 16, 512], BF16)

                def fn(t, blk=blk, bk=bk, jb=jb):
                    ins = None
                    for j in range(4):
                        q = jb * 4 + j
                        for kc in range(KC):
                            ins = mm(t, bk[:, q * 2:(q + 1) * 2], blk[:, kc, j * 128:(j + 1) * 128],
                                     sc[:, kc * 2:(kc + 1) * 2], kc == 0, kc == KC - 1)
                    return ins
                P.pe(fn, reads=(bkk, ("sc",)), writes=(bkey,))
            for s in range(2):
                P.dve(lambda v, l=l, s=s, bk=bk: v.tensor_tensor(
                    out=mod[l][:, :, s], in0=bk[:, 0:288].rearrange("p (q s) -> p q s", s=2)[:, :, s],
                    in1=pkv("ada_b", l * 144, (l + 1) * 144), op=ALU.add),
                    reads=(bkey, ("pk",)), writes=(("mod", l),))
        for s in range(2):
            for l in range(2):
                for i in range(3):
                    idx = (s * 2 + l) * 3 + i
                    P.dve(lambda v, s=s, l=l, i=i, idx=idx: v.scalar_tensor_tensor(
                        out=Aall[:, idx, :], in0=mod[l][:, (i * 3 + 1) * 16:(i * 3 + 2) * 16, s], scalar=1.0,
                        in1=pkv("norm_g", (l * 3 + i) * 16, (l * 3 + i + 1) * 16), op0=ALU.add, op1=ALU.mult),
                        reads=(("mod", l), ("pk",)), writes=(("consts",),))
                    P.dve(lambda v, s=s, l=l, i=i, idx=idx: v.tensor_scalar(
                        out=Gall[:, idx, :], in0=mod[l][:, (i * 3 + 2) * 16:(i * 3 + 3) * 16, s],
                        scalar1=(1.0 if i == 1 else 0.5), scalar2=None, op0=ALU.mult),
                        reads=(("mod", l),), writes=(("consts",),))
            P.dve(lambda v, s=s: v.tensor_tensor(out=sgall[:, s, :], in0=Gall[:, (s * 2 + 1) * 3 + 1, :],
                                                 in1=pkv("c_scale"), op=ALU.mult),
                  reads=(("consts",), ("pk",)), writes=(("consts",),))
            P.dve(lambda v, s=s: v.tensor_tensor(out=bsgall[:, s, :], in0=sgall[:, s, :], in1=pkv("c_b"), op=ALU.mult),
                  reads=(("consts",), ("pk",)), writes=(("consts",),))

        HK = [("h", dc) for dc in range(KC)]
        XNK = [("xn", kc) for kc in range(KC)]

        def live(tl):
            return [sg for sg in tl["segs"] if sg["alive"]]

        def tblocks(tl):
            out = []
            for sg in sorted(live(tl), key=lambda g: g["c0"]):
                TB = min(sg["n"], 128)
                for b in range(sg["n"] // TB):
                    out.append((sg["c0"] + b * TB, TB, sg, b))
            return out

        def load_x(tl):
            for sg in tl["segs"]:
                TB = min(sg["n"], 128)
                for tb in range(sg["n"] // TB):
                    c0 = sg["c0"] + tb * TB
                    for q in range(4):
                        xb, xk = xs.next()
                        P.dma("sp", xb[0:TB, :], sg["x"][tb * TB:(tb + 1) * TB, q * 512:(q + 1) * 512], f"x{xk[1]}",
                              writes=(xk,))
                        bk, bkey = PS.next()
                        P.pe(lambda t, xb=xb, bk=bk, TB=TB: [
                            t.transpose(bk[:, j * TB:(j + 1) * TB], xb[0:TB, j * 128:(j + 1) * 128], ident[0:TB, 0:TB])
                            for j in range(4)][-1], reads=(xk, ("ident",)), writes=(bkey,))
                        P.act(lambda a, bk=bk, q=q, c0=c0, TB=TB: a.copy(
                            out=h[:, 4 * q:4 * q + 4, c0:c0 + TB],
                            in_=bk[:, 0:4 * TB].rearrange("p (j t) -> p j t", j=4)),
                            reads=(bkey,), writes=tuple(("h", 4 * q + j) for j in range(4)))

        def norm_stats(tl):
            T = tl["T"]
            P.fence_big()
            P.act(lambda a: a.activation(out=sq[:, :, :T], in_=h[:, :, :T], func=AF.Square), reads=HK, writes=(("B:sq",),))
            P.dve(lambda v: v.tensor_reduce(out=sqs[:, :T], in_=sq[:, :, :T].rearrange("p k t -> p t k"), axis=AX.X,
                                            op=ALU.add), reads=(("B:sq",),), writes=(("sqs",),))
            bk, bkey = PS.next()
            P.pe(lambda t: mm(t, bk[:, :T], ones_f, sqs[:, :T], True, True), reads=(("sqs",), ("ones",)), writes=(bkey,))
            P.act(lambda a: a.activation(out=sqs[:, :T], in_=bk[:, :T], func=AF.Sqrt, scale=1.0 / D, bias=eps_t[:, 0:1]),
                  reads=(bkey, ("eps",)), writes=(("sqs",),))
            P.dve(lambda v: v.reciprocal(out=rstd[:, :T], in_=sqs[:, :T]), reads=(("sqs",),), writes=(("rstd",),))

        def ssq_update(tl, dc):
            T = tl["T"]
            ts, tsk = TS.next()
            P.act(lambda a, ts=ts, dc=dc: a.activation(out=ts[:, :T], in_=h[:, dc, :T], func=AF.Square),
                  reads=(("h", dc),), writes=(tsk,))
            if tl["ssq_n"] == 0:
                P.dve(lambda v, ts=ts: v.tensor_copy(out=ssq[:, :T], in_=ts[:, :T]), reads=(tsk,), writes=(("ssq",),))
            else:
                P.dve(lambda v, ts=ts: v.tensor_tensor(out=ssq[:, :T], in0=ssq[:, :T], in1=ts[:, :T], op=ALU.add),
                      reads=(tsk, ("ssq",)), writes=(("ssq",),))
            tl["ssq_n"] += 1

        def norm_stats_inc(tl):
            T = tl["T"]
            assert tl["ssq_n"] == KC, tl["ssq_n"]
            tl["ssq_n"] = 0
            bk, bkey = PS.next()
            P.pe(lambda t: mm(t, bk[:, :T], ones_f, ssq[:, :T], True, True), reads=(("ssq",), ("ones",)), writes=(bkey,))
            P.act(lambda a: a.activation(out=sqs[:, :T], in_=bk[:, :T], func=AF.Sqrt, scale=1.0 / D, bias=eps_t[:, 0:1]),
                  reads=(bkey, ("eps",)), writes=(("sqs",),))
            P.dve(lambda v: v.reciprocal(out=rstd[:, :T], in_=sqs[:, :T]), reads=(("sqs",),), writes=(("rstd",),))

        def stats(tl):
            if tl["ssq_n"] == KC:
                norm_stats_inc(tl)
            else:
                assert tl["ssq_n"] == 0
                norm_stats(tl)

        def cidx(s, l, i):
            return (s * 2 + l) * 3 + i

        def norm_mod(tl, l, i):
            stats(tl)
            for dc in range(KC):
                for sg in live(tl):
                    s, c0, c1 = sg["s"], sg["c0"], sg["c0"] + sg["n"]
                    ci = cidx(s, l, i)
                    tm, tk = TM.next()
                    P.dve(lambda v, dc=dc, tm=tm, ci=ci, c0=c0, c1=c1: v.scalar_tensor_tensor(
                        out=tm[:, c0:c1], in0=h[:, dc, c0:c1], scalar=Aall[:, ci, dc:dc + 1], in1=rstd[:, c0:c1],
                        op0=ALU.mult, op1=ALU.mult), reads=(("h", dc), ("rstd",), ("consts",)), writes=(tk,))
                    P.act(lambda a, dc=dc, tm=tm, s=s, c0=c0, c1=c1: a.activation(
                        out=xn[:, dc, c0:c1], in_=tm[:, c0:c1], func=AF.Identity,
                        bias=mod[l][:, (i * 3) * 16 + dc, s:s + 1], scale=1.0),
                        reads=(tk, ("mod", l)), writes=(("xn", dc),))

        def residual(tl, l, i, bb, bbk, dc):
            for sg in live(tl):
                s, c0, c1 = sg["s"], sg["c0"], sg["c0"] + sg["n"]
                ci = cidx(s, l, i)
                P.dve(lambda v, bb=bb, dc=dc, ci=ci, c0=c0, c1=c1: v.scalar_tensor_tensor(
                    out=h[:, dc, c0:c1], in0=bb[:, c0:c1], scalar=Gall[:, ci, dc:dc + 1], in1=h[:, dc, c0:c1],
                    op0=ALU.mult, op1=ALU.add), reads=(bbk, ("h", dc), ("consts",)), writes=(("h", dc),))
            ssq_update(tl, dc)

        def ffn(tl, l, f):
            i = 0 if f == 0 else 2
            norm_mod(tl, l, i)
            T = tl["T"]
            P.fence_big()
            for mg in range(11):
                gv, gk = wget(wg_v[l][f][:, :, mg * 512:(mg + 1) * 512], [128, 16, 512], BF16)
                uv, uk = wget(wu_v[l][f][:, :, mg * 512:(mg + 1) * 512], [128, 16, 512], BF16)
                for m4 in range(4):
                    m = mg * 4 + m4
                    bg_, bgk = PS.next()
                    bu_, buk = PS.next()
                    P.pe(lambda t, bg_=bg_, gv=gv, m4=m4: [
                        mm(t, bg_[:, :T], gv[:, kc, m4 * 128:(m4 + 1) * 128], xn[:, kc, :T], kc == 0, kc == KC - 1)
                        for kc in range(KC)][-1], reads=[gk] + XNK, writes=(bgk,))
                    P.pe(lambda t, bu_=bu_, uv=uv, m4=m4: [
                        mm(t, bu_[:, :T], uv[:, kc, m4 * 128:(m4 + 1) * 128], xn[:, kc, :T], kc == 0, kc == KC - 1)
                        for kc in range(KC)][-1], reads=[uk] + XNK, writes=(buk,))
                    ts, tsk = TS.next()
                    P.act(lambda a, ts=ts, bg_=bg_: a.activation(out=ts[:, :T], in_=bg_[:, :T], func=AF.Silu),
                          reads=(bgk,), writes=(tsk,))
                    P.dve(lambda v, ts=ts, bu_=bu_, m=m: v.tensor_tensor(out=act[:, m, :T], in0=bu_[:, :T], in1=ts[:, :T],
                                                                       op=ALU.mult),
                          reads=(buk, tsk), writes=(("B:act", m),))
            for dp in range(8):
                b0, b0k = PS.next()
                b1, b1k = PS.next()
                for mh in range(2):
                    dv, dk = wget(wd_v[l][f][:, mh * 22:(mh + 1) * 22, dp * 256:(dp + 1) * 256], [128, 22, 256], BF16)

                    def fn(t, dv=dv, mh=mh, b0=b0, b1=b1):
                        ins = None
                        for mm_ in range(22):
                            m = mh * 22 + mm_
                            for dci, bb in enumerate((b0, b1)):
                                ins = mm(t, bb[:, :T], dv[:, mm_, dci * 128:(dci + 1) * 128], act[:, m, :T],
                                         m == 0, m == MC - 1)
                        return ins
                    P.pe(fn, reads=[dk] + [("B:act", mh * 22 + k) for k in range(22)], writes=(b0k, b1k))
                for dci, (bb, bbk) in enumerate(((b0, b0k), (b1, b1k))):
                    residual(tl, l, i, bb, bbk, 2 * dp + dci)

        def out_state(src, nchunk, nrow, stage, dst, keys):
            for q in range(nchunk // 4):
                bk, bkey = PS.next()
                P.pe(lambda t, bk=bk, q=q: [
                    t.transpose(bk[0:nrow, j * 128:(j + 1) * 128], src[:, 4 * q + j, :], ident) for j in range(4)][-1],
                    reads=list(keys) + [("ident",)], writes=(bkey,))
                P.act(lambda a, bk=bk, q=q: a.copy(out=stage[0:nrow, q * 512:(q + 1) * 512], in_=bk[0:nrow, :]),
                      reads=(bkey,), writes=(("B:stage", id(stage)),))
            P.dma("sp", dst[:, :], stage[0:nrow, 0:nchunk * 128], "ost", reads=(("B:stage", id(stage)),))

        def mixer_ab(tl):
            l = 0
            norm_mod(tl, l, 1)
            T = tl["T"]
            P.fence_big()
            dw = pkv("dw").rearrange("p (c k) -> p c k", c=8)
            flag = pkv("flag")
            for half in range(2):
                bav, bak = wget(win_v[:, :, (4 + half) * 512:(5 + half) * 512], [128, 16, 512], BF16)
                bgv, bgk = wget(win_v[:, :, (6 + half) * 512:(7 + half) * 512], [128, 16, 512], BF16)
                for c4 in range(4):
                    cc = half * 4 + c4
                    pa, pak = PS.next()
                    pg, pgk = PS.next()
                    P.pe(lambda t, pa=pa, bav=bav, c4=c4: [
                        mm(t, pa[:, :T], bav[:, kc, c4 * 128:(c4 + 1) * 128], xn[:, kc, :T], kc == 0, kc == KC - 1)
                        for kc in range(KC)][-1], reads=[bak] + XNK, writes=(pak,))
                    P.pe(lambda t, pg=pg, bgv=bgv, c4=c4: [
                        mm(t, pg[:, :T], bgv[:, kc, c4 * 128:(c4 + 1) * 128], xn[:, kc, :T], kc == 0, kc == KC - 1)
                        for kc in range(KC)][-1], reads=[bgk] + XNK, writes=(pgk,))
                    ts, tsk = TS.next()
                    P.act(lambda a, ts=ts, pg=pg: a.activation(out=ts[:, :T], in_=pg[:, :T], func=AF.Sigmoid),
                          reads=(pgk,), writes=(tsk,))
                    for sg in live(tl):
                        s, c0, n = sg["s"], sg["c0"], sg["n"]
                        chk = ("ch", s, cc)
                        gl, glk = glu.next()
                        P.act(lambda a, gl=gl, cc=cc, s=s: a.copy(out=gl[:, 0:30], in_=convhist[s][:, cc, :]),
                              reads=(chk,), writes=(glk,))
                        P.dve(lambda v, gl=gl, pa=pa, ts=ts, c0=c0, n=n: v.tensor_tensor(
                            out=gl[:, 30:30 + n], in0=pa[:, c0:c0 + n], in1=ts[:, c0:c0 + n], op=ALU.mult),
                            reads=(pak, tsk, glk), writes=(glk,))
                        if sg["kind"] == "halo":
                            P.act(lambda a, gl=gl, cc=cc, s=s, n=n: a.activation(
                                out=convhist[s][:, cc, :], in_=gl[:, n:n + 30], func=AF.Identity, scale=flag[:, 0:1]),
                                reads=(glk, chk, ("pk",)), writes=(chk,))
                        else:
                            P.act(lambda a, gl=gl, cc=cc, s=s, n=n: a.copy(out=convhist[s][:, cc, :], in_=gl[:, n:n + 30]),
                                  reads=(glk, chk), writes=(chk,))
                        ak = ("B:acc", cc, c0)
                        for k in range(31):
                            if k == 0:
                                P.dve(lambda v, gl=gl, cc=cc, c0=c0, n=n: v.tensor_scalar(
                                    out=acc[:, cc, c0:c0 + n], in0=gl[:, 0:n], scalar1=dw[:, cc, 0:1], scalar2=None,
                                    op0=ALU.mult), reads=(glk, ("pk",)), writes=(ak,))
                            else:
                                P.dve(lambda v, gl=gl, cc=cc, k=k, c0=c0, n=n: v.scalar_tensor_tensor(
                                    out=acc[:, cc, c0:c0 + n], in0=gl[:, k:k + n], scalar=dw[:, cc, k:k + 1],
                                    in1=acc[:, cc, c0:c0 + n], op0=ALU.mult, op1=ALU.add),
                                    reads=(glk, ak), writes=(ak,))
            def acck(cc):
                return [("B:acc", cc, sg["c0"]) for sg in live(tl)]
            pm, pmk = PS.next()
            pq, pqk = PS.next()
            for cc in range(8):
                ts, tsk = TS.next()
                P.act(lambda a, ts=ts, cc=cc: a.activation(out=ts[:, :T], in_=acc[:, cc, :T], func=AF.Square),
                      reads=acck(cc), writes=(tsk,))
                P.pe(lambda t, cc=cc: mm(t, pm[:, :T], ones_f, acc[:, cc, :T], cc == 0, cc == 7),
                     reads=acck(cc) + [("ones",)], writes=(pmk,))
                P.pe(lambda t, cc=cc, ts=ts: mm(t, pq[:, :T], ones_f, ts[:, :T], cc == 0, cc == 7),
                     reads=(tsk, ("ones",)), writes=(pqk,))
            mean, mk = glu.aps[0][:, 0:512], ("B:glu", 0)
            lrs, lk = glu.aps[1][:, 0:512], ("B:glu", 1)
            P.dve(lambda v: v.tensor_scalar(out=mean[:, :T], in0=pm[:, :T], scalar1=1.0 / 1024, scalar2=None, op0=ALU.mult),
                  reads=(pmk,), writes=(mk,))
            P.dve(lambda v: v.tensor_tensor(out=lrs[:, :T], in0=mean[:, :T], in1=mean[:, :T], op=ALU.mult),
                  reads=(mk,), writes=(lk,))
            P.dve(lambda v: v.scalar_tensor_tensor(out=lrs[:, :T], in0=pq[:, :T], scalar=1.0 / 1024, in1=lrs[:, :T],
                                                   op0=ALU.mult, op1=ALU.subtract), reads=(pqk, lk), writes=(lk,))
            P.act(lambda a: a.activation(out=lrs[:, :T], in_=lrs[:, :T], func=AF.Sqrt, bias=eps_t[:, 0:1], scale=1.0),
                  reads=(lk, ("eps",)), writes=(lk,))
            P.dve(lambda v: v.reciprocal(out=lrs[:, :T], in_=lrs[:, :T]), reads=(lk,), writes=(lk,))
            for cc in range(8):
                tm, tk = TM.next()
                P.dve(lambda v, tm=tm, cc=cc: v.tensor_tensor(out=tm[:, :T], in0=acc[:, cc, :T], in1=mean[:, :T],
                                                             op=ALU.subtract), reads=acck(cc) + [mk], writes=(tk,))
                P.dve(lambda v, tm=tm: v.tensor_tensor(out=tm[:, :T], in0=tm[:, :T], in1=lrs[:, :T], op=ALU.mult),
                      reads=(tk, lk), writes=(tk,))
                P.act(lambda a, tm=tm, cc=cc: a.activation(out=ab[:, 8 + cc, :T], in_=tm[:, :T], func=AF.Silu,
                                                          scale=pkv("ln_g")[:, cc:cc + 1], bias=pkv("ln_b")[:, cc:cc + 1]),
                      reads=(tk, ("pk",)), writes=(("B:ab", 8 + cc),))
            v0, v0k = wget(win_v[:, :, 2 * 512:3 * 512], [128, 16, 512], BF16)
            v1, v1k = wget(win_v[:, :, 3 * 512:4 * 512], [128, 16, 512], BF16)
            TBL = tblocks(tl)
            for tb, (c0, TB, sg, b) in enumerate(TBL):
                vgt, vgk = vg.next()
                for hv, (vv, vk) in enumerate(((v0, v0k), (v1, v1k))):
                    pv, pvk = PS.next()
                    P.pe(lambda t, pv=pv, vv=vv, c0=c0, TB=TB: [
                        mm(t, pv[0:TB, :], xn[:, kc, c0:c0 + TB], vv[:, kc, :], kc == 0, kc == KC - 1)
                        for kc in range(KC)][-1], reads=[vk] + XNK, writes=(pvk,))
                    P.act(lambda a, pv=pv, vgt=vgt, hv=hv, TB=TB: a.activation(out=vgt[0:TB, hv * 512:(hv + 1) * 512],
                                                                            in_=pv[0:TB, :], func=AF.Gelu_apprx_tanh),
                          reads=(pvk,), writes=(vgk,))
                st, stk = TM.next()
                ts, tsk = TS.next()
                P.dve(lambda v, vgt=vgt, st=st, TB=TB: v.tensor_reduce(out=st[0:TB, 0:1], in_=vgt[0:TB, :], axis=AX.X, op=ALU.add),
                      reads=(vgk,), writes=(stk,))
                P.act(lambda a, vgt=vgt, ts=ts, TB=TB: a.activation(out=ts[0:TB, :], in_=vgt[0:TB, 0:512], func=AF.Square),
                      reads=(vgk,), writes=(tsk,))
                P.dve(lambda v, ts=ts, st=st, TB=TB: v.tensor_reduce(out=st[0:TB, 1:2], in_=ts[0:TB, :], axis=AX.X, op=ALU.add),
                      reads=(tsk, stk), writes=(stk,))
                P.act(lambda a, vgt=vgt, ts=ts, TB=TB: a.activation(out=ts[0:TB, :], in_=vgt[0:TB, 512:1024], func=AF.Square),
                      reads=(vgk, tsk), writes=(tsk,))
                P.dve(lambda v, ts=ts, st=st, TB=TB: v.tensor_reduce(out=st[0:TB, 4:5], in_=ts[0:TB, :], axis=AX.X, op=ALU.add),
                      reads=(tsk, stk), writes=(stk,))
                P.dve(lambda v, st=st, TB=TB: v.tensor_tensor(out=st[0:TB, 1:2], in0=st[0:TB, 1:2], in1=st[0:TB, 4:5], op=ALU.add),
                      reads=(stk,), writes=(stk,))
                P.dve(lambda v, st=st, TB=TB: v.tensor_scalar(out=st[0:TB, 2:3], in0=st[0:TB, 0:1], scalar1=1.0 / 1024, scalar2=None,
                                                              op0=ALU.mult), reads=(stk,), writes=(stk,))
                P.dve(lambda v, st=st, TB=TB: v.tensor_tensor(out=st[0:TB, 3:4], in0=st[0:TB, 2:3], in1=st[0:TB, 2:3], op=ALU.mult),
                      reads=(stk,), writes=(stk,))
                P.dve(lambda v, st=st, TB=TB: v.scalar_tensor_tensor(out=st[0:TB, 3:4], in0=st[0:TB, 1:2], scalar=1.0 / 1024,
                                                                     in1=st[0:TB, 3:4], op0=ALU.mult, op1=ALU.subtract),
                      reads=(stk,), writes=(stk,))
                P.act(lambda a, st=st, TB=TB: a.activation(out=st[0:TB, 3:4], in_=st[0:TB, 3:4], func=AF.Sqrt, bias=eps_t[0:TB, 0:1],
                                                           scale=1.0), reads=(stk, ("eps",)), writes=(stk,))
                P.dve(lambda v, st=st, TB=TB: v.reciprocal(out=st[0:TB, 3:4], in_=st[0:TB, 3:4]), reads=(stk,), writes=(stk,))
                P.dve(lambda v, vgt=vgt, st=st, TB=TB: v.tensor_scalar(out=vgt[0:TB, :], in0=vgt[0:TB, :], scalar1=st[0:TB, 2:3],
                                                                       scalar2=st[0:TB, 3:4], op0=ALU.subtract, op1=ALU.mult),
                      reads=(vgk, stk), writes=(vgk,))
                P.dve(lambda v, vgt=vgt, TB=TB: v.tensor_tensor(out=vgt[0:TB, :], in0=vgt[0:TB, :], in1=vgb[0:TB, 0:1024], op=ALU.mult),
                      reads=(vgk, ("vgb",)), writes=(vgk,))
                P.dve(lambda v, vgt=vgt, TB=TB: v.tensor_tensor(out=vgt[0:TB, :], in0=vgt[0:TB, :], in1=vgb[0:TB, 1024:2048], op=ALU.add),
                      reads=(vgk, ("vgb",)), writes=(vgk,))
                if sg["kind"] == "sample":
                    P.dma("sp", nav[b * TB:(b + 1) * TB, :], vgt[0:TB, :], "ost", reads=(vgk,))
                P.act(lambda a, vgt=vgt, tb=tb, TB=TB: a.copy(out=vnb[0:TB, tb, :], in_=vgt[0:TB, :]), reads=(vgk,),
                      writes=(("B:vnb", tb),))
            for half in range(2):
                uv, uk = wget(win_v[:, :, half * 512:(half + 1) * 512], [128, 16, 512], BF16)
                for c4 in range(4):
                    cc = half * 4 + c4
                    g = cc // 2
                    pu_, puk = PS.next()
                    P.pe(lambda t, pu_=pu_, uv=uv, c4=c4: [
                        mm(t, pu_[:, :T], uv[:, kc, c4 * 128:(c4 + 1) * 128], xn[:, kc, :T], kc == 0, kc == KC - 1)
                        for kc in range(KC)][-1], reads=[uk] + XNK, writes=(puk,))
                    ts, tsk = TS.next()
                    P.act(lambda a, ts=ts, pu_=pu_: a.activation(out=ts[:, :T], in_=pu_[:, :T], func=AF.Gelu_apprx_tanh),
                          reads=(puk,), writes=(tsk,))
                    pgt, pgtk = PS.next()

                    def fn(t, pgt=pgt, cc=cc, g=g):
                        ins = None
                        for tb, (c0, TB, sg, b) in enumerate(TBL):
                            o = pgt[:, c0:c0 + TB]
                            mm(t, o, vnb[0:TB, tb, cc * 128:(cc + 1) * 128], WmT[0:TB, g, 0:TB], True, False)
                            ins = mm(t, o, ones_f[0:1, :], abs_t[0:1, g * 128:g * 128 + TB], False, True)
                        return ins
                    P.pe(fn, reads=[("B:vnb", tb) for tb in range(len(TBL))] + [("WmT",), ("abs",), ("ones",)], writes=(pgtk,))
                    P.dve(lambda v, ts=ts, pgt=pgt, cc=cc: v.tensor_tensor(out=ab[:, cc, :T], in0=pgt[:, :T], in1=ts[:, :T],
                                                                         op=ALU.mult),
                          reads=(pgtk, tsk), writes=(("B:ab", cc),))
            ABK = [("B:ab", k) for k in range(16)]
            for dp in range(8):
                ov, ok = wget(wout_v[:, :, dp * 256:(dp + 1) * 256], [128, 16, 256], BF16)
                b0, b0k = PS.next()
                b1, b1k = PS.next()

                def fn(t, ov=ov, b0=b0, b1=b1):
                    ins = None
                    for kc in range(16):
                        for dci, bb in enumerate((b0, b1)):
                            ins = mm(t, bb[:, :T], ov[:, kc, dci * 128:(dci + 1) * 128], ab[:, kc, :T], kc == 0, kc == 15)
                    return ins
                P.pe(fn, reads=[ok] + ABK, writes=(b0k, b1k))
                for dci, (bb, bbk) in enumerate(((b0, b0k), (b1, b1k))):
                    residual(tl, l, 1, bb, bbk, 2 * dp + dci)

        def mixer_c(tl):
            l = 1
            stats(tl)
            P.fence_big()
            flag = pkv("flag")
            segs = live(tl)
            for g in range(4):
                w = 2 ** (g + 1)
                cv = ck = None
                if any(sg["kind"] != "halo" for sg in segs):
                    cv, ck = wget(cw_v[g], [128, 4, 512], BF16)
                for sg in segs:
                    s, c0, n = sg["s"], sg["c0"], sg["n"]
                    L = 15 + n
                    ci = cidx(s, l, 1)
                    phk = ("ph", s, g)
                    hm, hmk = HMG.next()
                    P.act(lambda a, hm=hm, g=g, s=s: a.copy(out=hm[:, :, 0:15], in_=poolhist[s][:, 4 * g:4 * g + 4, :]),
                          reads=(phk,), writes=(hmk,))
                    for j in range(4):
                        dc = 4 * g + j
                        tm, tk = TM.next()
                        P.dve(lambda v, dc=dc, tm=tm, ci=ci, c0=c0, n=n: v.scalar_tensor_tensor(
                            out=tm[:, 0:n], in0=h[:, dc, c0:c0 + n], scalar=Aall[:, ci, dc:dc + 1], in1=rstd[:, c0:c0 + n],
                            op0=ALU.mult, op1=ALU.mult), reads=(("h", dc), ("rstd",), ("consts",)), writes=(tk,))
                        P.act(lambda a, dc=dc, tm=tm, hm=hm, j=j, s=s, n=n: a.activation(
                            out=hm[:, j, 15:15 + n], in_=tm[:, 0:n], func=AF.Identity,
                            bias=mod[l][:, 3 * 16 + dc, s:s + 1], scale=1.0), reads=(tk, ("mod", l), hmk), writes=(hmk,))
                    if sg["kind"] == "halo":
                        P.act(lambda a, hm=hm, g=g, s=s, n=n: a.activation(
                            out=poolhist[s][:, 4 * g:4 * g + 4, :], in_=hm[:, :, n:n + 15], func=AF.Identity,
                            scale=flag[:, 0:1]), reads=(hmk, ("pk",)), writes=(phk,))
                        continue
                    P.act(lambda a, hm=hm, g=g, s=s, n=n: a.copy(out=poolhist[s][:, 4 * g:4 * g + 4, :], in_=hm[:, :, n:n + 15]),
                          reads=(hmk,), writes=(phk,))
                    srcs = [hm, sA, sB, sA, sB]
                    skeys = [hmk, ("B:sA",), ("B:sB",), ("B:sA",), ("B:sB",)]
                    for lev in range(1, g + 2):
                        sh = 2 ** (lev - 1)
                        lo = 2 ** lev - 1
                        P.dve(lambda v, a=srcs[lev - 1], o=srcs[lev], sh=sh, lo=lo, L=L: v.tensor_tensor(
                            out=o[:, :, lo:L], in0=a[:, :, lo:L], in1=a[:, :, lo - sh:L - sh], op=ALU.add),
                            reads=(skeys[lev - 1],), writes=(skeys[lev],))
                    sres, sresk = srcs[g + 1], skeys[g + 1]
                    pl, plk = PL.next()
                    P.dve(lambda v, sres=sres, hm=hm, pl=pl, w=w, L=L, n=n: v.scalar_tensor_tensor(
                        out=pl[:, :, 0:n], in0=sres[:, :, 15:L], scalar=1.0 / w, in1=hm[:, :, 15:L], op0=ALU.mult,
                        op1=ALU.subtract), reads=(sresk, hmk), writes=(plk,))
                    if sg["first"]:
                        iv = pkv("invc").rearrange("p (g j t) -> p g j t", g=4, j=4)[:, g, :, :]
                        tm, tk = TM.next()
                        tmv = tm[:, 0:64].rearrange("p (j t) -> p j t", j=4)
                        P.dve(lambda v, sres=sres, tmv=tmv, iv=iv: v.tensor_tensor(out=tmv, in0=sres[:, :, 15:31], in1=iv,
                                                                                 op=ALU.mult),
                              reads=(sresk, ("pk",)), writes=(tk,))
                        P.dve(lambda v, tmv=tmv, hm=hm, pl=pl: v.tensor_tensor(out=pl[:, :, 0:16], in0=tmv, in1=hm[:, :, 15:31],
                                                                             op=ALU.subtract),
                              reads=(tk, hmk, plk), writes=(plk,))
                    for jo in range(4):
                        dc = 4 * g + jo
                        bk, bkey = PS.next()
                        P.pe(lambda t, bk=bk, cv=cv, jo=jo, pl=pl, n=n: [
                            mm(t, bk[:, 0:n], cv[:, kc, jo * 128:(jo + 1) * 128], pl[:, kc, 0:n], kc == 0, kc == 3)
                            for kc in range(4)][-1], reads=(ck, plk), writes=(bkey,))
                        tm, tk = TM.next()
                        P.act(lambda a, bk=bk, tm=tm, dc=dc, s=s, n=n: a.activation(
                            out=tm[:, 0:n], in_=bk[:, 0:n], func=AF.Identity, scale=sgall[:, s, dc:dc + 1],
                            bias=bsgall[:, s, dc:dc + 1]), reads=(bkey, ("consts",)), writes=(tk,))
                        P.dve(lambda v, tm=tm, dc=dc, c0=c0, n=n: v.tensor_tensor(
                            out=h[:, dc, c0:c0 + n], in0=h[:, dc, c0:c0 + n], in1=tm[:, 0:n], op=ALU.add),
                            reads=(tk, ("h", dc)), writes=(("h", dc),))
            for sg in segs:
                if sg["kind"] == "halo":
                    sg["alive"] = False
            lv = live(tl)
            if lv:
                tl["T"] = max(sg["c0"] + sg["n"] for sg in lv)
                for dc in range(KC):
                    ssq_update(tl, dc)

        def final(tl):
            T = tl["T"]
            stats(tl)
            P.fence_big()
            fg = pkv("final_g")
            for dc in range(KC):
                P.dve(lambda v, dc=dc: v.scalar_tensor_tensor(
                    out=yn[:, dc, :T], in0=h[:, dc, :T], scalar=fg[:, dc:dc + 1], in1=rstd[:, :T], op0=ALU.mult,
                    op1=ALU.mult), reads=(("h", dc), ("rstd",), ("pk",)), writes=(("B:yn", dc),))
            for (c0, TB, sg, b) in tblocks(tl):
                for q in range(2):
                    ysb, ysk = YS.next()
                    for hb in range(2):
                        bk, bkey = PS.next()
                        P.pe(lambda t, bk=bk, q=q, hb=hb, c0=c0, TB=TB: [
                            t.transpose(bk[0:TB, j * 128:(j + 1) * 128], yn[:, 8 * q + 4 * hb + j, c0:c0 + TB], ident)
                            for j in range(4)][-1],
                            reads=[("B:yn", 8 * q + 4 * hb + j) for j in range(4)] + [("ident",)], writes=(bkey,))
                        P.act(lambda a, bk=bk, ysb=ysb, hb=hb, TB=TB: a.copy(out=ysb[0:TB, hb * 512:(hb + 1) * 512], in_=bk[0:TB, :]),
                              reads=(bkey,), writes=(ysk,))
                    P.dma("sp", sg["y"][b * TB:(b + 1) * TB, q * 1024:(q + 1) * 1024], ysb[0:TB, :], f"y{ysk[1]}",
                          reads=(ysk,))

        def run_tile(tl):
            tl["ssq_n"] = 0
            tl["T"] = max(sg["c0"] + sg["n"] for sg in tl["segs"])
            for sg in tl["segs"]:
                sg["alive"] = True
            load_x(tl)
            ffn(tl, 0, 0)
            mixer_ab(tl)
            ffn(tl, 0, 1)
            ffn(tl, 1, 0)
            mixer_c(tl)
            if not live(tl):
                return
            ffn(tl, 1, 1)
            final(tl)
            for sg in live(tl):
                if sg["last"]:
                    s = sg["s"]
                    out_state(convhist[s], 8, 30, ncst, ncp if s == 0 else ncs, [("ch", s, cc) for cc in range(8)])
                    out_state(poolhist[s], 16, 15, npst, npp if s == 0 else nps, [("ph", s, g) for g in range(4)])

        P0 = 256
        rem = TOK - P0
        if rem >= 768:
            assert (rem - 768) % 512 == 0
            sizes = [512] * ((rem - 768) // 512) + [384, 384]
        else:
            sizes = [rem] if rem > 0 else []
        seg_h = dict(kind="halo", s=0, c0=P0 + 64, n=128, x=xh, y=None, first=False, last=False)
        seg_s = dict(kind="sample", s=1, c0=P0, n=64, x=xsm, y=ysm, first=False, last=True)
        seg_p = dict(kind="prompt", s=0, c0=0, n=P0, x=xp[0:P0, :], y=yp[0:P0, :], first=True, last=(rem == 0))
        run_tile(dict(segs=[seg_h, seg_s, seg_p]))
        pos = P0
        for it, n in enumerate(sizes):
            run_tile(dict(segs=[dict(kind="prompt", s=0, c0=0, n=n, x=xp[pos:pos + n, :], y=yp[pos:pos + n, :],
                                     first=False, last=(it == len(sizes) - 1))]))
            pos += n

        sem_names = [s for s in P.cnt]
        sems = {}
        for sname in sem_names:
            sems[sname] = es.enter_context(nc.semaphore(f"sem_{sname}"))
        block = es.enter_context(nc.Block())
        P.emit(nc, block, sems)
    return nc


def _fm(v, chunks):
    return np.ascontiguousarray(np.asarray(v, np.float32).reshape(chunks, 128).T)


def make_in_maps(inp, NT):
    SEQH = NT * 512
    f32 = np.float32
    import os as _os
    if _os.environ.get("MK_TINYW", "") == "1":
        shared = dict(
            wsm=np.ascontiguousarray(inp["ffn_w_gate"][0, 0], f32),
            aws=np.ascontiguousarray(inp["a_ws"][0], f32),
            abs=np.ascontiguousarray(np.asarray(inp["a_bs"][0], f32).reshape(1, 512)),
            rows=np.ascontiguousarray(np.concatenate([inp["a_v_norm_g"][0], inp["a_v_norm_b"][0]]).reshape(1, 2048), f32),
        )
    else:
      shared = dict(
        ada_w=np.ascontiguousarray(inp["ada_w"], f32),
        wg=np.ascontiguousarray(inp["ffn_w_gate"], f32), wu=np.ascontiguousarray(inp["ffn_w_up"], f32),
        wd=np.ascontiguousarray(inp["ffn_w_down"], f32), win=np.ascontiguousarray(inp["ab_w_in"], f32),
        wout=np.ascontiguousarray(inp["ab_w_out"], f32), cw=np.ascontiguousarray(inp["c_w_grp"], f32),
        aws=np.ascontiguousarray(inp["a_ws"][0], f32),
        abs=np.ascontiguousarray(np.asarray(inp["a_bs"][0], f32).reshape(1, 512)),
        rows=np.ascontiguousarray(np.concatenate([inp["a_v_norm_g"][0], inp["a_v_norm_b"][0]]).reshape(1, 2048), f32),
    )
    pk_sh = np.zeros((128, NPK), f32)

    def put(pkarr, name, val):
        o, w = PK[name]
        pkarr[:, o:o + w] = np.asarray(val, f32).reshape(128, w)

    put(pk_sh, "ada_b", np.concatenate([_fm(inp["ada_b"][l], 144) for l in range(2)], axis=1))
    put(pk_sh, "norm_g", np.concatenate([_fm(inp["norm_g"][l, i], 16) for l in range(2) for i in range(3)], axis=1))
    put(pk_sh, "final_g", _fm(inp["final_norm_g"], 16))
    put(pk_sh, "dw", np.asarray(inp["b_dw"][0], f32).T.reshape(8, 128, 31).transpose(1, 0, 2))
    put(pk_sh, "ln_g", _fm(inp["b_ln_g"][0], 8))
    put(pk_sh, "ln_b", _fm(inp["b_ln_b"][0], 8))
    put(pk_sh, "c_b", _fm(np.asarray(inp["c_b_grp"][0]).reshape(-1), 16))
    put(pk_sh, "c_scale", _fm(inp["c_scale"][0], 16))
    maps = []
    for c in range(NCORES):
        bp, half, bs = c // 2, c % 2, c
        m = dict(shared)
        m["xp"] = np.ascontiguousarray(inp["x_prompt"][bp, half * SEQH:(half + 1) * SEQH], f32)
        m["xh"] = (np.ascontiguousarray(inp["x_prompt"][bp, SEQH - 128:SEQH], f32) if half == 1
                   else np.zeros((128, D), f32))
        m["xsm"] = np.ascontiguousarray(inp["x_sample"][bs], f32)
        pkc = pk_sh.copy()
        c2 = np.stack([np.asarray(inp["c_prompt"][bp], f32), np.asarray(inp["c_sample"][bs], f32)], axis=-1)
        put(pkc, "c2T", c2.reshape(16, 128, 2).transpose(1, 0, 2))
        put(pkc, "sconv", np.asarray(inp["state_conv_b"][0, bs], f32).T.reshape(8, 128, 30).transpose(1, 0, 2))
        put(pkc, "spool", np.asarray(inp["state_pool_c"][0, bs], f32).T.reshape(16, 128, 15).transpose(1, 0, 2))
        put(pkc, "flag", np.full((128, 1), float(half), f32))
        pos = half * SEQH + np.arange(16)
        iv = np.stack([1.0 / np.minimum(2 ** (g + 1), pos + 1) for g in range(4)]).astype(f32)
        put(pkc, "invc", np.broadcast_to(iv[None, :, None, :], (128, 4, 4, 16)))
        m["pk"] = pkc
        maps.append(m)
    return maps


_NC_CACHE = {}


def run(inp, NT):
    if NT not in _NC_CACHE:
        _NC_CACHE[NT] = build(NT)
    nc = _NC_CACHE[NT]
    maps = make_in_maps(inp, NT)
    res = run_bass_kernel_spmd(nc, maps, core_ids=list(range(NCORES)))
    R = res.results
    SEQH = NT * 512
    y_prompt = np.stack([np.concatenate([R[2 * b]["yp"], R[2 * b + 1]["yp"]], axis=0) for b in range(4)])
    y_sample = np.stack([R[c]["ysm"] for c in range(8)])
    ncp = np.stack([R[2 * b + 1]["ncp"] for b in range(4)])[None]
    ncs = np.stack([R[c]["ncs"] for c in range(8)])[None]
    npp = np.stack([R[2 * b + 1]["npp"] for b in range(4)])[None]
    nps = np.stack([R[c]["nps"] for c in range(8)])[None]
    nav = np.stack([R[c]["nav"] for c in range(8)])[None]
    return tuple(np.ascontiguousarray(a, dtype=np.float32) for a in (y_prompt, y_sample, ncp, ncs, npp, nps, nav))


def kernel(**inputs):
    inp = {k: np.asarray(v) for k, v in inputs.items()}
    return run(inp, 8)
```

```python
import numpy as np
from contextlib import ExitStack
import concourse.bass as bass
import concourse.mybir as mybir
from concourse.bass_utils import run_bass_kernel_spmd

F32 = mybir.dt.float32
BF16 = mybir.dt.bfloat16
AF = mybir.ActivationFunctionType
ALU = mybir.AluOpType
AX = mybir.AxisListType

D = 2048
DFF = 5632
KC = 16
MC = 44
EPS = 1e-6
NCORES = 8
SLOT_W = 4096
NS = 4
BIGW = 13376

PK = {}
_o = 0
for _n, _w in [("ada_b", 288), ("norm_g", 96), ("final_g", 16), ("dw", 248), ("ln_g", 8), ("ln_b", 8),
               ("c_b", 16), ("c_scale", 16), ("c2T", 32), ("sconv", 240), ("spool", 240), ("flag", 1),
               ("invc", 256)]:
    PK[_n] = (_o, _w)
    _o += _w
NPK = _o


class Prog:
    ENGS = ("pe", "act", "dve", "pool", "sp")

    def __init__(self):
        self.q = {e: [] for e in self.ENGS}
        self.cnt = {}
        self.keys = {}
        self.big_base = {}

    def _entry(self, k):
        e = self.keys.get(k)
        if e is None:
            base = dict(self.big_base) if (isinstance(k[0], str) and k[0].startswith("B:")) else {}
            e = [None, base]
            self.keys[k] = e
        return e

    def fence_big(self):
        merged = dict(self.big_base)
        for k in [k for k in self.keys if isinstance(k[0], str) and k[0].startswith("B:")]:
            w, rd = self.keys.pop(k)
            if w is not None:
                merged[w[0]] = max(merged.get(w[0], 0), w[1])
            for s, v in rd.items():
                merged[s] = max(merged.get(s, 0), v)
        self.big_base = merged

    def _record(self, eng, fn, reads, writes, sem, inc):
        waits = {}

        def need(tok):
            if tok is not None:
                waits[tok[0]] = max(waits.get(tok[0], 0), tok[1])

        for k in reads:
            need(self._entry(k)[0])
        for k in writes:
            e = self._entry(k)
            need(e[0])
            for s, v in e[1].items():
                need((s, v))
        if eng == "pe":
            waits.pop("pe", None)
        self.cnt[sem] = self.cnt.get(sem, 0) + inc
        tok = (sem, self.cnt[sem])
        self.q[eng].append((fn, waits, (sem, inc)))
        for k in reads:
            e = self._entry(k)
            e[1][sem] = max(e[1].get(sem, 0), tok[1])
        for k in writes:
            self.keys[k] = [tok, {}]
        return tok

    def op(self, eng, fn, reads=(), writes=()):
        return self._record(eng, fn, reads, writes, eng, 1)

    def pe(self, fn, reads=(), writes=()):
        return self.op("pe", fn, reads, writes)

    def act(self, fn, reads=(), writes=()):
        return self.op("act", fn, reads, writes)

    def dve(self, fn, reads=(), writes=()):
        return self.op("dve", fn, reads, writes)

    def pool(self, fn, reads=(), writes=()):
        return self.op("pool", fn, reads, writes)

    def dma(self, queue, out, in_, sem, reads=(), writes=()):
        return self._record(queue, lambda e: e.dma_start(out=out, in_=in_), reads, writes, sem, 16)

    def emit(self, nc, block, sems):
        prog = self

        def run(eng_name, eng):
            waited = {}
            for fn, waits, (sem, inc) in prog.q[eng_name]:
                for s in sorted(waits):
                    v = waits[s]
                    if waited.get(s, 0) < v:
                        eng.wait_ge(sems[s], v)
                        waited[s] = v
                ins = fn(eng)
                ins.then_inc(sems[sem], inc)
            if eng_name == "sp":
                for s, v in prog.cnt.items():
                    if s not in prog.ENGS and waited.get(s, 0) < v:
                        eng.wait_ge(sems[s], v)

        @block.tensor
        def _(t):
            run("pe", t)

        @block.scalar
        def _(a):
            run("act", a)

        @block.vector
        def _(v):
            run("dve", v)

        @block.gpsimd
        def _(g):
            run("pool", g)

        @block.sync
        def _(s):
            run("sp", s)


class Rot:
    def __init__(self, name, aps):
        self.aps = aps
        self.name = name
        self.i = 0

    def next(self):
        i = self.i % len(self.aps)
        self.i += 1
        return self.aps[i], (self.name, i)


def build(NT):
    TOK = NT * 512
    nc = bass.Bass("TRN2", target_bir_lowering=False)
    P = Prog()

    def din(name, shape):
        return nc.dram_tensor(name, list(shape), F32, kind="ExternalInput").ap()

    def dout(name, shape):
        return nc.dram_tensor(name, list(shape), F32, kind="ExternalOutput").ap()

    xp = din("xp", [TOK, D]); xh = din("xh", [128, D]); xsm = din("xsm", [64, D])
    pk_d = din("pk", [128, NPK]); rows_d = din("rows", [1, 2048]); abs_d = din("abs", [1, 512])
    aws_d = din("aws", [4, 128, 128])
    import os as _os
    TINYW = _os.environ.get("MK_TINYW", "") == "1"
    if not TINYW:
        ada_w = din("ada_w", [2, D, 9 * D])
        wg = din("wg", [2, 2, D, DFF]); wu = din("wu", [2, 2, D, DFF]); wd = din("wd", [2, 2, DFF, D])
        win = din("win", [1, D, 4096]); wout = din("wout", [1, D, D]); cw = din("cw", [1, 4, 512, 512])
    else:
        wsm = din("wsm", [D, DFF])
    yp = dout("yp", [TOK, D]); ysm = dout("ysm", [64, D])
    ncp = dout("ncp", [30, 1024]); ncs = dout("ncs", [30, 1024])
    npp = dout("npp", [15, D]); nps = dout("nps", [15, D]); nav = dout("nav", [64, 1024])

    with ExitStack() as es:
        def sb(name, shape, dt=F32):
            return es.enter_context(nc.sbuf_tensor(name, list(shape), dt))

        h_t = sb("h", [128, KC, 512]); h = h_t[:]
        xn_t = sb("xn", [128, KC, 512], BF16); xn = xn_t[:]
        big_t = sb("big", [128, BIGW])
        slots = [sb(f"slot{i}", [128, SLOT_W]) for i in range(NS)]
        xs = Rot("xs", [sb(f"xs{i}", [128, 512])[:] for i in range(2)])
        acc2 = xs.aps[0]
        TS = Rot("ts", [sb(f"ts{i}", [128, 512])[:] for i in range(2)])
        TM = Rot("tm", [sb(f"tmm{i}", [128, 512])[:] for i in range(2)])
        sqs = sb("sqs", [128, 512])[:]
        rstd = sb("rstd", [128, 512])[:]
        pk = sb("pkt", [128, NPK])[:]
        vgb = sb("vgb", [128, 2048])[:]
        abs_t = sb("abst", [1, 512])[:]
        ident = sb("ident", [128, 128])[:]
        ones_f = sb("onesf", [128, 128])[:]
        eps_t = sb("epst", [128, 1])[:]
        WmT = sb("WmT", [128, 4, 128], BF16)[:]
        sc = sb("sc", [128, 32], BF16)[:]
        ssq = sb("ssq", [128, 512])[:]
        mod = [sb(f"mod{l}", [128, 144, 2])[:] for l in range(2)]
        Aall = sb("Aall", [128, 12, 16])[:]
        Gall = sb("Gall", [128, 12, 16])[:]
        sgall = sb("sgall", [128, 2, 16])[:]
        bsgall = sb("bsgall", [128, 2, 16])[:]
        convhist = [sb(f"convhist{s}", [128, 8, 30])[:] for s in range(2)]
        poolhist = [sb(f"poolhist{s}", [128, 16, 15])[:] for s in range(2)]
        banks = [es.enter_context(nc.psum_tensor(f"bank{i}", [128, 512], F32))[:] for i in range(8)]
        PS = Rot("ps", banks)

        big = big_t[:]

        def bview(off, words, dt=F32, pattern=None, **kw):
            v = big[:, off:off + words]
            if dt == BF16:
                v = v.bitcast(BF16)
            if pattern:
                v = v.rearrange(pattern, **kw)
            return v

        act = bview(0, 11264, BF16, "p (m t) -> p m t", m=MC)
        sq = bview(0, 8192, F32, "p (k t) -> p k t", k=KC)
        vg = Rot("B:vg", [bview(0, 1024), bview(1024, 1024)])
        vnb = bview(2048, 2048, BF16, "p (b c) -> p b c", b=4)
        glu = Rot("B:glu", [bview(4096, 544), bview(4640, 544)])
        acc = bview(5184, 4096, F32, "p (c t) -> p c t", c=8)
        ab = bview(9280, 4096, BF16, "p (k t) -> p k t", k=16)
        HMG = Rot("B:hmg", [bview(0, 2112, F32, "p (j t) -> p j t", j=4), bview(2112, 2112, F32, "p (j t) -> p j t", j=4)])
        sA = bview(4224, 2112, F32, "p (j t) -> p j t", j=4)
        sB = bview(6336, 2112, F32, "p (j t) -> p j t", j=4)
        PL = Rot("B:pl", [bview(8448, 1024, BF16, "p (j t) -> p j t", j=4), bview(9472, 1024, BF16, "p (j t) -> p j t", j=4)])
        yn = bview(0, 8192, F32, "p (k t) -> p k t", k=KC)
        YS = Rot("B:ys", [bview(8192, 1024), bview(9216, 1024)])
        ncst = bview(10240, 1024)
        npst = bview(11264, 2048)
        aws_st = bview(0, 512, F32, "p (g j) -> p g j", g=4)

        def pkv(name, a=0, b=None):
            o, w = PK[name]
            b = w if b is None else b
            return pk[:, o + a:o + b]

        ring_i = [0]

        def wget(src, shape, dt):
            s = ring_i[0] % NS
            ring_i[0] += 1
            words = int(np.prod(shape[1:])) // (2 if dt == BF16 else 1)
            v = slots[s][:, 0:words]
            if dt == BF16:
                v = v.bitcast(BF16)
            if len(shape) == 3:
                v = v.rearrange("p (a b) -> p a b", a=shape[1])
            key = ("slot", s)
            P.dma("pool", v, src, f"w{s}", reads=(), writes=(key,))
            return v, key

        if not TINYW:
            wg_v = [[wg[l, f].rearrange("(kc p) n -> p kc n", p=128) for f in range(2)] for l in range(2)]
            wu_v = [[wu[l, f].rearrange("(kc p) n -> p kc n", p=128) for f in range(2)] for l in range(2)]
            wd_v = [[wd[l, f].rearrange("(m p) n -> p m n", p=128) for f in range(2)] for l in range(2)]
            win_v = win[0].rearrange("(kc p) n -> p kc n", p=128)
            wout_v = wout[0].rearrange("(kc p) n -> p kc n", p=128)
            cw_v = [cw[0, g].rearrange("(kc p) n -> p kc n", p=128) for g in range(4)]
            ada_v = [ada_w[l].rearrange("(kc p) n -> p kc n", p=128) for l in range(2)]
        else:
            w_kc = wsm.rearrange("(kc p) n -> p kc n", p=128)
            wg_v = [[w_kc for f in range(2)] for l in range(2)]
            wu_v = wg_v
            w_d = wsm.rearrange("a b -> (a b)").rearrange("(r c) -> r c", c=D).rearrange("(m p) n -> p m n", p=128)
            wd_v = [[w_d for f in range(2)] for l in range(2)]
            win_v = w_kc
            wout_v = w_kc
            cw_v = [wsm[0:512, g * 512:(g + 1) * 512].rearrange("(kc p) n -> p kc n", p=128) for g in range(4)]

            class _AdaV:
                def __getitem__(self, idx):
                    sl = idx[2]
                    jb = (sl.start // 512) % 11
                    return w_kc[:, :, jb * 512:(jb + 1) * 512]
            ada_v = [_AdaV(), _AdaV()]

        def mm(t, out, lhsT, rhs, start, stop):
            return t.matmul(out, lhsT, rhs, start=start, stop=stop)

        P.dma("sp", pk, pk_d[:, :], "c0", writes=(("pk",),))
        P.dma("sp", vgb, rows_d[0:1, :].partition_broadcast(128), "c1", writes=(("vgb",),))
        P.dma("sp", abs_t, abs_d[:, :], "c2", writes=(("abs",),))
        P.fence_big()
        P.dma("sp", aws_st, aws_d.rearrange("g i j -> i g j"), "c3", writes=(("B:aws",),))
        P.pool(lambda g: g.memset(ident, 0.0), writes=(("ident",),))
        P.pool(lambda g: g.affine_select(out=ident, in_=ident, compare_op=ALU.not_equal, fill=1.0, base=0,
                                         pattern=[[-1, 128]], channel_multiplier=1),
               reads=(("ident",),), writes=(("ident",),))
        P.dve(lambda v: v.memset(ones_f, 1.0), writes=(("ones",),))
        P.dve(lambda v: v.memset(eps_t, EPS), writes=(("eps",),))
        P.dve(lambda v: v.memset(convhist[0], 0.0), writes=tuple(("ch", 0, cc) for cc in range(8)))
        P.dve(lambda v: v.memset(poolhist[0], 0.0), writes=tuple(("ph", 0, g) for g in range(4)))
        P.dve(lambda v: v.tensor_copy(out=convhist[1], in_=pkv("sconv").rearrange("p (c k) -> p c k", c=8)),
              reads=(("pk",),), writes=tuple(("ch", 1, cc) for cc in range(8)))
        P.dve(lambda v: v.tensor_copy(out=poolhist[1], in_=pkv("spool").rearrange("p (c k) -> p c k", c=16)),
              reads=(("pk",),), writes=tuple(("ph", 1, g) for g in range(4)))
        bk, bkey = PS.next()
        P.pe(lambda t: [t.transpose(bk[:, g * 128:(g + 1) * 128], aws_st[:, g, :], ident) for g in range(4)][-1],
             reads=(("B:aws",), ("ident",)), writes=(bkey,))
        P.act(lambda a: a.copy(out=WmT, in_=bk.rearrange("p (g i) -> p g i", g=4)), reads=(bkey,), writes=(("WmT",),))
        P.dve(lambda v: v.memset(WmT[64:128, :, 0:64], 0.0), reads=(("WmT",),), writes=(("WmT",),))
        P.act(lambda a: a.activation(out=sc, in_=pkv("c2T"), func=AF.Silu), reads=(("pk",),), writes=(("sc",),))
        for l in range(2):
            bk, bkey = PS.next()
            for jb in range(36):
                blk, bkk = wget(ada_v[l][:, :, jb * 512:(jb + 1) * 512], [128, 16, 512], BF16)

                def fn(t, blk=blk, bk=bk, jb=jb):
                    ins = None
                    for j in range(4):
                        q = jb * 4 + j
                        for kc in range(KC):
                            ins = mm(t, bk[:, q * 2:(q + 1) * 2], blk[:, kc, j * 128:(j + 1) * 128],
                                     sc[:, kc * 2:(kc + 1) * 2], kc == 0, kc == KC - 1)
                    return ins
                P.pe(fn, reads=(bkk, ("sc",)), writes=(bkey,))
            for s in range(2):
                P.dve(lambda v, l=l, s=s, bk=bk: v.tensor_tensor(
                    out=mod[l][:, :, s], in0=bk[:, 0:288].rearrange("p (q s) -> p q s", s=2)[:, :, s],
                    in1=pkv("ada_b", l * 144, (l + 1) * 144), op=ALU.add),
                    reads=(bkey, ("pk",)), writes=(("mod", l),))
        for s in range(2):
            for l in range(2):
                for i in range(3):
                    idx = (s * 2 + l) * 3 + i
                    P.dve(lambda v, s=s, l=l, i=i, idx=idx: v.scalar_tensor_tensor(
                        out=Aall[:, idx, :], in0=mod[l][:, (i * 3 + 1) * 16:(i * 3 + 2) * 16, s], scalar=1.0,
                        in1=pkv("norm_g", (l * 3 + i) * 16, (l * 3 + i + 1) * 16), op0=ALU.add, op1=ALU.mult),
                        reads=(("mod", l), ("pk",)), writes=(("consts",),))
                    P.dve(lambda v, s=s, l=l, i=i, idx=idx: v.tensor_scalar(
                        out=Gall[:, idx, :], in0=mod[l][:, (i * 3 + 2) * 16:(i * 3 + 3) * 16, s],
                        scalar1=(1.0 if i == 1 else 0.5), scalar2=None, op0=ALU.mult),
                        reads=(("mod", l),), writes=(("consts",),))
            P.dve(lambda v, s=s: v.tensor_tensor(out=sgall[:, s, :], in0=Gall[:, (s * 2 + 1) * 3 + 1, :],
                                                 in1=pkv("c_scale"), op=ALU.mult),
                  reads=(("consts",), ("pk",)), writes=(("consts",),))
            P.dve(lambda v, s=s: v.tensor_tensor(out=bsgall[:, s, :], in0=sgall[:, s, :], in1=pkv("c_b"), op=ALU.mult),
                  reads=(("consts",), ("pk",)), writes=(("consts",),))

        HK = [("h", dc) for dc in range(KC)]
        XNK = [("xn", kc) for kc in range(KC)]

        def live(tl):
            return [sg for sg in tl["segs"] if sg["alive"]]

        def tblocks(tl):
            out = []
            for sg in sorted(live(tl), key=lambda g: g["c0"]):
                TB = min(sg["n"], 128)
                for b in range(sg["n"] // TB):
                    out.append((sg["c0"] + b * TB, TB, sg, b))
            return out

        def load_x(tl):
            for sg in tl["segs"]:
                TB = min(sg["n"], 128)
                for tb in range(sg["n"] // TB):
                    c0 = sg["c0"] + tb * TB
                    for q in range(4):
                        xb, xk = xs.next()
                        P.dma("sp", xb[0:TB, :], sg["x"][tb * TB:(tb + 1) * TB, q * 512:(q + 1) * 512], f"x{xk[1]}",
                              writes=(xk,))
                        bk, bkey = PS.next()
                        P.pe(lambda t, xb=xb, bk=bk, TB=TB: [
                            t.transpose(bk[:, j * TB:(j + 1) * TB], xb[0:TB, j * 128:(j + 1) * 128], ident[0:TB, 0:TB])
                            for j in range(4)][-1], reads=(xk, ("ident",)), writes=(bkey,))
                        P.act(lambda a, bk=bk, q=q, c0=c0, TB=TB: a.copy(
                            out=h[:, 4 * q:4 * q + 4, c0:c0 + TB],
                            in_=bk[:, 0:4 * TB].rearrange("p (j t) -> p j t", j=4)),
                            reads=(bkey,), writes=tuple(("h", 4 * q + j) for j in range(4)))

        def norm_stats(tl):
            T = tl["T"]
            P.fence_big()
            for (c0, TB, sg, b) in tblocks(tl):
                P.act(lambda a, c0=c0, TB=TB: a.activation(out=sq[:, :, c0:c0 + TB], in_=h[:, :, c0:c0 + TB], func=AF.Square),
                      reads=HK, writes=(("B:sq", c0),))
                P.dve(lambda v, c0=c0, TB=TB: v.tensor_reduce(
                    out=sqs[:, c0:c0 + TB], in_=sq[:, :, c0:c0 + TB].rearrange("p k t -> p t k"), axis=AX.X, op=ALU.add),
                    reads=(("B:sq", c0),), writes=(("sqs",),))
            bk, bkey = PS.next()
            P.pe(lambda t: mm(t, bk[:, :T], ones_f, sqs[:, :T], True, True), reads=(("sqs",), ("ones",)), writes=(bkey,))
            P.act(lambda a: a.activation(out=sqs[:, :T], in_=bk[:, :T], func=AF.Sqrt, scale=1.0 / D, bias=eps_t[:, 0:1]),
                  reads=(bkey, ("eps",)), writes=(("sqs",),))
            P.dve(lambda v: v.reciprocal(out=rstd[:, :T], in_=sqs[:, :T]), reads=(("sqs",),), writes=(("rstd",),))

        def ssq_update(tl, dc):
            T = tl["T"]
            ts, tsk = TS.next()
            P.act(lambda a, ts=ts, dc=dc: a.activation(out=ts[:, :T], in_=h[:, dc, :T], func=AF.Square),
                  reads=(("h", dc),), writes=(tsk,))
            if tl["ssq_n"] == 0:
                P.dve(lambda v, ts=ts: v.tensor_copy(out=ssq[:, :T], in_=ts[:, :T]), reads=(tsk,), writes=(("ssq",),))
            else:
                P.dve(lambda v, ts=ts: v.tensor_tensor(out=ssq[:, :T], in0=ssq[:, :T], in1=ts[:, :T], op=ALU.add),
                      reads=(tsk, ("ssq",)), writes=(("ssq",),))
            tl["ssq_n"] += 1

        def norm_stats_inc(tl):
            T = tl["T"]
            assert tl["ssq_n"] == KC, tl["ssq_n"]
            tl["ssq_n"] = 0
            bk, bkey = PS.next()
            P.pe(lambda t: mm(t, bk[:, :T], ones_f, ssq[:, :T], True, True), reads=(("ssq",), ("ones",)), writes=(bkey,))
            P.act(lambda a: a.activation(out=sqs[:, :T], in_=bk[:, :T], func=AF.Sqrt, scale=1.0 / D, bias=eps_t[:, 0:1]),
                  reads=(bkey, ("eps",)), writes=(("sqs",),))
            P.dve(lambda v: v.reciprocal(out=rstd[:, :T], in_=sqs[:, :T]), reads=(("sqs",),), writes=(("rstd",),))

        def stats(tl):
            if tl["ssq_n"] == KC:
                norm_stats_inc(tl)
            else:
                assert tl["ssq_n"] == 0
                norm_stats(tl)

        def cidx(s, l, i):
            return (s * 2 + l) * 3 + i

        def norm_mod(tl, l, i):
            stats(tl)
            for dc in range(KC):
                for sg in live(tl):
                    s, c0, c1 = sg["s"], sg["c0"], sg["c0"] + sg["n"]
                    ci = cidx(s, l, i)
                    tm, tk = TM.next()
                    P.dve(lambda v, dc=dc, tm=tm, ci=ci, c0=c0, c1=c1: v.scalar_tensor_tensor(
                        out=tm[:, c0:c1], in0=h[:, dc, c0:c1], scalar=Aall[:, ci, dc:dc + 1], in1=rstd[:, c0:c1],
                        op0=ALU.mult, op1=ALU.mult), reads=(("h", dc), ("rstd",), ("consts",)), writes=(tk,))
                    P.act(lambda a, dc=dc, tm=tm, s=s, c0=c0, c1=c1: a.activation(
                        out=xn[:, dc, c0:c1], in_=tm[:, c0:c1], func=AF.Identity,
                        bias=mod[l][:, (i * 3) * 16 + dc, s:s + 1], scale=1.0),
                        reads=(tk, ("mod", l)), writes=(("xn", dc),))

        def residual(tl, l, i, bb, bbk, dc):
            for sg in live(tl):
                s, c0, c1 = sg["s"], sg["c0"], sg["c0"] + sg["n"]
                ci = cidx(s, l, i)
                P.dve(lambda v, bb=bb, dc=dc, ci=ci, c0=c0, c1=c1: v.scalar_tensor_tensor(
                    out=h[:, dc, c0:c1], in0=bb[:, c0:c1], scalar=Gall[:, ci, dc:dc + 1], in1=h[:, dc, c0:c1],
                    op0=ALU.mult, op1=ALU.add), reads=(bbk, ("h", dc), ("consts",)), writes=(("h", dc),))
            ssq_update(tl, dc)

        def ffn(tl, l, f):
            i = 0 if f == 0 else 2
            norm_mod(tl, l, i)
            T = tl["T"]
            P.fence_big()
            for mg in range(11):
                gv, gk = wget(wg_v[l][f][:, :, mg * 512:(mg + 1) * 512], [128, 16, 512], BF16)
                uv, uk = wget(wu_v[l][f][:, :, mg * 512:(mg + 1) * 512], [128, 16, 512], BF16)
                for m4 in range(4):
                    m = mg * 4 + m4
                    bg_, bgk = PS.next()
                    bu_, buk = PS.next()
                    P.pe(lambda t, bg_=bg_, gv=gv, m4=m4: [
                        mm(t, bg_[:, :T], gv[:, kc, m4 * 128:(m4 + 1) * 128], xn[:, kc, :T], kc == 0, kc == KC - 1)
                        for kc in range(KC)][-1], reads=[gk] + XNK, writes=(bgk,))
                    P.pe(lambda t, bu_=bu_, uv=uv, m4=m4: [
                        mm(t, bu_[:, :T], uv[:, kc, m4 * 128:(m4 + 1) * 128], xn[:, kc, :T], kc == 0, kc == KC - 1)
                        for kc in range(KC)][-1], reads=[uk] + XNK, writes=(buk,))
                    ts, tsk = TS.next()
                    P.act(lambda a, ts=ts, bg_=bg_: a.activation(out=ts[:, :T], in_=bg_[:, :T], func=AF.Silu),
                          reads=(bgk,), writes=(tsk,))
                    P.dve(lambda v, ts=ts, bu_=bu_, m=m: v.tensor_tensor(out=act[:, m, :T], in0=bu_[:, :T], in1=ts[:, :T],
                                                                       op=ALU.mult),
                          reads=(buk, tsk), writes=(("B:act", m),))
            for dp in range(8):
                b0, b0k = PS.next()
                b1, b1k = PS.next()
                for mh in range(2):
                    dv, dk = wget(wd_v[l][f][:, mh * 22:(mh + 1) * 22, dp * 256:(dp + 1) * 256], [128, 22, 256], BF16)

                    def fn(t, dv=dv, mh=mh, b0=b0, b1=b1):
                        ins = None
                        for mm_ in range(22):
                            m = mh * 22 + mm_
                            for dci, bb in enumerate((b0, b1)):
                                ins = mm(t, bb[:, :T], dv[:, mm_, dci * 128:(dci + 1) * 128], act[:, m, :T],
                                         m == 0, m == MC - 1)
                        return ins
                    P.pe(fn, reads=[dk] + [("B:act", mh * 22 + k) for k in range(22)], writes=(b0k, b1k))
                for dci, (bb, bbk) in enumerate(((b0, b0k), (b1, b1k))):
                    residual(tl, l, i, bb, bbk, 2 * dp + dci)

        def out_state(src, nchunk, nrow, stage, dst, keys):
            for q in range(nchunk // 4):
                bk, bkey = PS.next()
                P.pe(lambda t, bk=bk, q=q: [
                    t.transpose(bk[0:nrow, j * 128:(j + 1) * 128], src[:, 4 * q + j, :], ident) for j in range(4)][-1],
                    reads=list(keys) + [("ident",)], writes=(bkey,))
                P.act(lambda a, bk=bk, q=q: a.copy(out=stage[0:nrow, q * 512:(q + 1) * 512], in_=bk[0:nrow, :]),
                      reads=(bkey,), writes=(("B:stage", id(stage)),))
            P.dma("sp", dst[:, :], stage[0:nrow, 0:nchunk * 128], "ost", reads=(("B:stage", id(stage)),))

        def mixer_ab(tl):
            l = 0
            norm_mod(tl, l, 1)
            T = tl["T"]
            P.fence_big()
            dw = pkv("dw").rearrange("p (c k) -> p c k", c=8)
            flag = pkv("flag")
            blk = {}
            TBL = tblocks(tl)

            def acck(cc):
                return [("B:acc", cc, sg["c0"]) for sg in live(tl)]

            def V():
                v0, v0k = wget(win_v[:, :, 2 * 512:3 * 512], [128, 16, 512], BF16)
                v1, v1k = wget(win_v[:, :, 3 * 512:4 * 512], [128, 16, 512], BF16)
                for tb, (c0, TB, sg, b) in enumerate(TBL):
                    vgt, vgk = vg.next()
                    for hv, (vv, vk) in enumerate(((v0, v0k), (v1, v1k))):
                        pv, pvk = PS.next()
                        P.pe(lambda t, pv=pv, vv=vv, c0=c0, TB=TB: [
                            mm(t, pv[0:TB, :], xn[:, kc, c0:c0 + TB], vv[:, kc, :], kc == 0, kc == KC - 1)
                            for kc in range(KC)][-1], reads=[vk] + XNK, writes=(pvk,))
                        P.act(lambda a, pv=pv, vgt=vgt, hv=hv, TB=TB: a.activation(out=vgt[0:TB, hv * 512:(hv + 1) * 512],
                                                                                in_=pv[0:TB, :], func=AF.Gelu_apprx_tanh),
                              reads=(pvk,), writes=(vgk,))
                    st, stk = TM.next()
                    ts, tsk = TS.next()
                    P.dve(lambda v, vgt=vgt, st=st, TB=TB: v.tensor_reduce(out=st[0:TB, 0:1], in_=vgt[0:TB, :], axis=AX.X, op=ALU.add),
                          reads=(vgk,), writes=(stk,))
                    P.act(lambda a, vgt=vgt, ts=ts, TB=TB: a.activation(out=ts[0:TB, :], in_=vgt[0:TB, 0:512], func=AF.Square),
                          reads=(vgk,), writes=(tsk,))
                    P.dve(lambda v, ts=ts, st=st, TB=TB: v.tensor_reduce(out=st[0:TB, 1:2], in_=ts[0:TB, :], axis=AX.X, op=ALU.add),
                          reads=(tsk, stk), writes=(stk,))
                    P.act(lambda a, vgt=vgt, ts=ts, TB=TB: a.activation(out=ts[0:TB, :], in_=vgt[0:TB, 512:1024], func=AF.Square),
                          reads=(vgk, tsk), writes=(tsk,))
                    P.dve(lambda v, ts=ts, st=st, TB=TB: v.tensor_reduce(out=st[0:TB, 4:5], in_=ts[0:TB, :], axis=AX.X, op=ALU.add),
                          reads=(tsk, stk), writes=(stk,))
                    P.dve(lambda v, st=st, TB=TB: v.tensor_tensor(out=st[0:TB, 1:2], in0=st[0:TB, 1:2], in1=st[0:TB, 4:5], op=ALU.add),
                          reads=(stk,), writes=(stk,))
                    P.dve(lambda v, st=st, TB=TB: v.tensor_scalar(out=st[0:TB, 2:3], in0=st[0:TB, 0:1], scalar1=1.0 / 1024, scalar2=None,
                                                                  op0=ALU.mult), reads=(stk,), writes=(stk,))
                    P.dve(lambda v, st=st, TB=TB: v.tensor_tensor(out=st[0:TB, 3:4], in0=st[0:TB, 2:3], in1=st[0:TB, 2:3], op=ALU.mult),
                          reads=(stk,), writes=(stk,))
                    P.dve(lambda v, st=st, TB=TB: v.scalar_tensor_tensor(out=st[0:TB, 3:4], in0=st[0:TB, 1:2], scalar=1.0 / 1024,
                                                                         in1=st[0:TB, 3:4], op0=ALU.mult, op1=ALU.subtract),
                          reads=(stk,), writes=(stk,))
                    P.act(lambda a, st=st, TB=TB: a.activation(out=st[0:TB, 3:4], in_=st[0:TB, 3:4], func=AF.Sqrt, bias=eps_t[0:TB, 0:1],
                                                               scale=1.0), reads=(stk, ("eps",)), writes=(stk,))
                    P.dve(lambda v, st=st, TB=TB: v.reciprocal(out=st[0:TB, 3:4], in_=st[0:TB, 3:4]), reads=(stk,), writes=(stk,))
                    P.dve(lambda v, vgt=vgt, st=st, TB=TB: v.tensor_scalar(out=vgt[0:TB, :], in0=vgt[0:TB, :], scalar1=st[0:TB, 2:3],
                                                                           scalar2=st[0:TB, 3:4], op0=ALU.subtract, op1=ALU.mult),
                          reads=(vgk, stk), writes=(vgk,))
                    P.dve(lambda v, vgt=vgt, TB=TB: v.tensor_tensor(out=vgt[0:TB, :], in0=vgt[0:TB, :], in1=vgb[0:TB, 0:1024], op=ALU.mult),
                          reads=(vgk, ("vgb",)), writes=(vgk,))
                    P.dve(lambda v, vgt=vgt, TB=TB: v.tensor_tensor(out=vgt[0:TB, :], in0=vgt[0:TB, :], in1=vgb[0:TB, 1024:2048], op=ALU.add),
                          reads=(vgk, ("vgb",)), writes=(vgk,))
                    if sg["kind"] == "sample":
                        P.dma("sp", nav[b * TB:(b + 1) * TB, :], vgt[0:TB, :], "ost", reads=(vgk,))
                    P.act(lambda a, vgt=vgt, tb=tb, TB=TB: a.copy(out=vnb[0:TB, tb, :], in_=vgt[0:TB, :]), reads=(vgk,),
                          writes=(("B:vnb", tb),))

            def B(cc):
                half, c4 = cc // 4, cc % 4
                if c4 == 0:
                    blk["ba"] = wget(win_v[:, :, (4 + half) * 512:(5 + half) * 512], [128, 16, 512], BF16)
                    blk["bg"] = wget(win_v[:, :, (6 + half) * 512:(7 + half) * 512], [128, 16, 512], BF16)
                bav, bak = blk["ba"]
                bgv, bgk = blk["bg"]
                pa, pak = PS.next()
                pg, pgk = PS.next()
                P.pe(lambda t, pa=pa, bav=bav, c4=c4: [
                    mm(t, pa[:, :T], bav[:, kc, c4 * 128:(c4 + 1) * 128], xn[:, kc, :T], kc == 0, kc == KC - 1)
                    for kc in range(KC)][-1], reads=[bak] + XNK, writes=(pak,))
                P.pe(lambda t, pg=pg, bgv=bgv, c4=c4: [
                    mm(t, pg[:, :T], bgv[:, kc, c4 * 128:(c4 + 1) * 128], xn[:, kc, :T], kc == 0, kc == KC - 1)
                    for kc in range(KC)][-1], reads=[bgk] + XNK, writes=(pgk,))
                ts, tsk = TS.next()
                P.act(lambda a, ts=ts, pg=pg: a.activation(out=ts[:, :T], in_=pg[:, :T], func=AF.Sigmoid),
                      reads=(pgk,), writes=(tsk,))
                for sg in live(tl):
                    s, c0, n = sg["s"], sg["c0"], sg["n"]
                    chk = ("ch", s, cc)
                    gl, glk = glu.next()
                    P.act(lambda a, gl=gl, cc=cc, s=s: a.copy(out=gl[:, 0:30], in_=convhist[s][:, cc, :]),
                          reads=(chk,), writes=(glk,))
                    P.dve(lambda v, gl=gl, pa=pa, ts=ts, c0=c0, n=n: v.tensor_tensor(
                        out=gl[:, 30:30 + n], in0=pa[:, c0:c0 + n], in1=ts[:, c0:c0 + n], op=ALU.mult),
                        reads=(pak, tsk, glk), writes=(glk,))
                    if sg["kind"] == "halo":
                        P.act(lambda a, gl=gl, cc=cc, s=s, n=n: a.activation(
                            out=convhist[s][:, cc, :], in_=gl[:, n:n + 30], func=AF.Identity, scale=flag[:, 0:1]),
                            reads=(glk, chk, ("pk",)), writes=(chk,))
                    else:
                        P.act(lambda a, gl=gl, cc=cc, s=s, n=n: a.copy(out=convhist[s][:, cc, :], in_=gl[:, n:n + 30]),
                              reads=(glk, chk), writes=(chk,))
                    ak = ("B:acc", cc, c0)
                    a2k = ("xs", 0)
                    for k in range(31):
                        dst, dk_ = (acc[:, cc, c0:c0 + n], ak) if k % 2 == 0 else (acc2[:, 0:n], a2k)
                        if k < 2:
                            P.dve(lambda v, gl=gl, cc=cc, k=k, n=n, dst=dst: v.tensor_scalar(
                                out=dst, in0=gl[:, k:k + n], scalar1=dw[:, cc, k:k + 1], scalar2=None,
                                op0=ALU.mult), reads=(glk, ("pk",)), writes=(dk_,))
                        else:
                            P.dve(lambda v, gl=gl, cc=cc, k=k, n=n, dst=dst: v.scalar_tensor_tensor(
                                out=dst, in0=gl[:, k:k + n], scalar=dw[:, cc, k:k + 1],
                                in1=dst, op0=ALU.mult, op1=ALU.add),
                                reads=(glk, dk_), writes=(dk_,))
                    P.dve(lambda v, cc=cc, c0=c0, n=n: v.tensor_tensor(
                        out=acc[:, cc, c0:c0 + n], in0=acc[:, cc, c0:c0 + n], in1=acc2[:, 0:n], op=ALU.add),
                        reads=(ak, a2k), writes=(ak,))

            def U(cc):
                half, c4 = cc // 4, cc % 4
                if c4 == 0:
                    blk["u"] = wget(win_v[:, :, half * 512:(half + 1) * 512], [128, 16, 512], BF16)
                uv, uk = blk["u"]
                g = cc // 2
                pu_, puk = PS.next()
                P.pe(lambda t, pu_=pu_, uv=uv, c4=c4: [
                    mm(t, pu_[:, :T], uv[:, kc, c4 * 128:(c4 + 1) * 128], xn[:, kc, :T], kc == 0, kc == KC - 1)
                    for kc in range(KC)][-1], reads=[uk] + XNK, writes=(puk,))
                ts, tsk = TS.next()
                P.act(lambda a, ts=ts, pu_=pu_: a.activation(out=ts[:, :T], in_=pu_[:, :T], func=AF.Gelu_apprx_tanh),
                      reads=(puk,), writes=(tsk,))
                pgt, pgtk = PS.next()

                def fn(t, pgt=pgt, cc=cc, g=g):
                    ins = None
                    for tb, (c0, TB, sg, b) in enumerate(TBL):
                        o = pgt[:, c0:c0 + TB]
                        mm(t, o, vnb[0:TB, tb, cc * 128:(cc + 1) * 128], WmT[0:TB, g, 0:TB], True, False)
                        ins = mm(t, o, ones_f[0:1, :], abs_t[0:1, g * 128:g * 128 + TB], False, True)
                    return ins
                P.pe(fn, reads=[("B:vnb", tb) for tb in range(len(TBL))] + [("WmT",), ("abs",), ("ones",)], writes=(pgtk,))
                P.dve(lambda v, ts=ts, pgt=pgt, cc=cc: v.tensor_tensor(out=ab[:, cc, :T], in0=pgt[:, :T], in1=ts[:, :T],
                                                                     op=ALU.mult),
                      reads=(pgtk, tsk), writes=(("B:ab", cc),))

            V()
            B(0); B(1)
            for cc in range(8):
                U(cc)
                if cc + 2 < 8:
                    B(cc + 2)

            pm, pmk = PS.next()
            pq, pqk = PS.next()
            for cc in range(8):
                ts, tsk = TS.next()
                P.act(lambda a, ts=ts, cc=cc: a.activation(out=ts[:, :T], in_=acc[:, cc, :T], func=AF.Square),
                      reads=acck(cc), writes=(tsk,))
                P.pe(lambda t, cc=cc: mm(t, pm[:, :T], ones_f, acc[:, cc, :T], cc == 0, cc == 7),
                     reads=acck(cc) + [("ones",)], writes=(pmk,))
                P.pe(lambda t, cc=cc, ts=ts: mm(t, pq[:, :T], ones_f, ts[:, :T], cc == 0, cc == 7),
                     reads=(tsk, ("ones",)), writes=(pqk,))
            mean, mk = glu.aps[0][:, 0:512], ("B:glu", 0)
            lrs, lk = glu.aps[1][:, 0:512], ("B:glu", 1)
            P.dve(lambda v: v.tensor_scalar(out=mean[:, :T], in0=pm[:, :T], scalar1=1.0 / 1024, scalar2=None, op0=ALU.mult),
                  reads=(pmk,), writes=(mk,))
            P.dve(lambda v: v.tensor_tensor(out=lrs[:, :T], in0=mean[:, :T], in1=mean[:, :T], op=ALU.mult),
                  reads=(mk,), writes=(lk,))
            P.dve(lambda v: v.scalar_tensor_tensor(out=lrs[:, :T], in0=pq[:, :T], scalar=1.0 / 1024, in1=lrs[:, :T],
                                                   op0=ALU.mult, op1=ALU.subtract), reads=(pqk, lk), writes=(lk,))
            P.act(lambda a: a.activation(out=lrs[:, :T], in_=lrs[:, :T], func=AF.Sqrt, bias=eps_t[:, 0:1], scale=1.0),
                  reads=(lk, ("eps",)), writes=(lk,))
            P.dve(lambda v: v.reciprocal(out=lrs[:, :T], in_=lrs[:, :T]), reads=(lk,), writes=(lk,))
            for cc in range(8):
                tm, tk = TM.next()
                P.dve(lambda v, tm=tm, cc=cc: v.tensor_tensor(out=tm[:, :T], in0=acc[:, cc, :T], in1=mean[:, :T],
                                                             op=ALU.subtract), reads=acck(cc) + [mk], writes=(tk,))
                P.dve(lambda v, tm=tm: v.tensor_tensor(out=tm[:, :T], in0=tm[:, :T], in1=lrs[:, :T], op=ALU.mult),
                      reads=(tk, lk), writes=(tk,))
                P.act(lambda a, tm=tm, cc=cc: a.activation(out=ab[:, 8 + cc, :T], in_=tm[:, :T], func=AF.Silu,
                                                          scale=pkv("ln_g")[:, cc:cc + 1], bias=pkv("ln_b")[:, cc:cc + 1]),
                      reads=(tk, ("pk",)), writes=(("B:ab", 8 + cc),))
            ABK = [("B:ab", k) for k in range(16)]
            for dp in range(8):
                ov, ok = wget(wout_v[:, :, dp * 256:(dp + 1) * 256], [128, 16, 256], BF16)
                b0, b0k = PS.next()
                b1, b1k = PS.next()

                def fn(t, ov=ov, b0=b0, b1=b1):
                    ins = None
                    for kc in range(16):
                        for dci, bb in enumerate((b0, b1)):
                            ins = mm(t, bb[:, :T], ov[:, kc, dci * 128:(dci + 1) * 128], ab[:, kc, :T], kc == 0, kc == 15)
                    return ins
                P.pe(fn, reads=[ok] + ABK, writes=(b0k, b1k))
                for dci, (bb, bbk) in enumerate(((b0, b0k), (b1, b1k))):
                    residual(tl, l, 1, bb, bbk, 2 * dp + dci)

        def mixer_c(tl):
            l = 1
            stats(tl)
            P.fence_big()
            flag = pkv("flag")
            segs = live(tl)
            for g in range(4):
                w = 2 ** (g + 1)
                cv = ck = None
                if any(sg["kind"] != "halo" for sg in segs):
                    cv, ck = wget(cw_v[g], [128, 4, 512], BF16)
                for sg in segs:
                    s, c0, n = sg["s"], sg["c0"], sg["n"]
                    L = 15 + n
                    ci = cidx(s, l, 1)
                    phk = ("ph", s, g)
                    hm, hmk = HMG.next()
                    P.act(lambda a, hm=hm, g=g, s=s: a.copy(out=hm[:, :, 0:15], in_=poolhist[s][:, 4 * g:4 * g + 4, :]),
                          reads=(phk,), writes=(hmk,))
                    for j in range(4):
                        dc = 4 * g + j
                        tm, tk = TM.next()
                        P.dve(lambda v, dc=dc, tm=tm, ci=ci, c0=c0, n=n: v.scalar_tensor_tensor(
                            out=tm[:, 0:n], in0=h[:, dc, c0:c0 + n], scalar=Aall[:, ci, dc:dc + 1], in1=rstd[:, c0:c0 + n],
                            op0=ALU.mult, op1=ALU.mult), reads=(("h", dc), ("rstd",), ("consts",)), writes=(tk,))
                        P.act(lambda a, dc=dc, tm=tm, hm=hm, j=j, s=s, n=n: a.activation(
                            out=hm[:, j, 15:15 + n], in_=tm[:, 0:n], func=AF.Identity,
                            bias=mod[l][:, 3 * 16 + dc, s:s + 1], scale=1.0), reads=(tk, ("mod", l), hmk), writes=(hmk,))
                    if sg["kind"] == "halo":
                        P.act(lambda a, hm=hm, g=g, s=s, n=n: a.activation(
                            out=poolhist[s][:, 4 * g:4 * g + 4, :], in_=hm[:, :, n:n + 15], func=AF.Identity,
                            scale=flag[:, 0:1]), reads=(hmk, ("pk",)), writes=(phk,))
                        continue
                    P.act(lambda a, hm=hm, g=g, s=s, n=n: a.copy(out=poolhist[s][:, 4 * g:4 * g + 4, :], in_=hm[:, :, n:n + 15]),
                          reads=(hmk,), writes=(phk,))
                    srcs = [hm, sA, sB, sA, sB]
                    skeys = [hmk, ("B:sA",), ("B:sB",), ("B:sA",), ("B:sB",)]
                    for lev in range(1, g + 2):
                        sh = 2 ** (lev - 1)
                        lo = 2 ** lev - 1
                        P.dve(lambda v, a=srcs[lev - 1], o=srcs[lev], sh=sh, lo=lo, L=L: v.tensor_tensor(
                            out=o[:, :, lo:L], in0=a[:, :, lo:L], in1=a[:, :, lo - sh:L - sh], op=ALU.add),
                            reads=(skeys[lev - 1],), writes=(skeys[lev],))
                    sres, sresk = srcs[g + 1], skeys[g + 1]
                    pl, plk = PL.next()
                    P.dve(lambda v, sres=sres, hm=hm, pl=pl, w=w, L=L, n=n: v.scalar_tensor_tensor(
                        out=pl[:, :, 0:n], in0=sres[:, :, 15:L], scalar=1.0 / w, in1=hm[:, :, 15:L], op0=ALU.mult,
                        op1=ALU.subtract), reads=(sresk, hmk), writes=(plk,))
                    if sg["first"]:
                        iv = pkv("invc").rearrange("p (g j t) -> p g j t", g=4, j=4)[:, g, :, :]
                        tm, tk = TM.next()
                        tmv = tm[:, 0:64].rearrange("p (j t) -> p j t", j=4)
                        P.dve(lambda v, sres=sres, tmv=tmv, iv=iv: v.tensor_tensor(out=tmv, in0=sres[:, :, 15:31], in1=iv,
                                                                                 op=ALU.mult),
                              reads=(sresk, ("pk",)), writes=(tk,))
                        P.dve(lambda v, tmv=tmv, hm=hm, pl=pl: v.tensor_tensor(out=pl[:, :, 0:16], in0=tmv, in1=hm[:, :, 15:31],
                                                                             op=ALU.subtract),
                              reads=(tk, hmk, plk), writes=(plk,))
                    for jo in range(4):
                        dc = 4 * g + jo
                        bk, bkey = PS.next()
                        P.pe(lambda t, bk=bk, cv=cv, jo=jo, pl=pl, n=n: [
                            mm(t, bk[:, 0:n], cv[:, kc, jo * 128:(jo + 1) * 128], pl[:, kc, 0:n], kc == 0, kc == 3)
                            for kc in range(4)][-1], reads=(ck, plk), writes=(bkey,))
                        tm, tk = TM.next()
                        P.act(lambda a, bk=bk, tm=tm, dc=dc, s=s, n=n: a.activation(
                            out=tm[:, 0:n], in_=bk[:, 0:n], func=AF.Identity, scale=sgall[:, s, dc:dc + 1],
                            bias=bsgall[:, s, dc:dc + 1]), reads=(bkey, ("consts",)), writes=(tk,))
                        P.dve(lambda v, tm=tm, dc=dc, c0=c0, n=n: v.tensor_tensor(
                            out=h[:, dc, c0:c0 + n], in0=h[:, dc, c0:c0 + n], in1=tm[:, 0:n], op=ALU.add),
                            reads=(tk, ("h", dc)), writes=(("h", dc),))
            for sg in segs:
                if sg["kind"] == "halo":
                    sg["alive"] = False
            lv = live(tl)
            if lv:
                tl["T"] = max(sg["c0"] + sg["n"] for sg in lv)
                for dc in range(KC):
                    ssq_update(tl, dc)

        def final(tl):
            T = tl["T"]
            stats(tl)
            P.fence_big()
            fg = pkv("final_g")
            for dc in range(KC):
                P.dve(lambda v, dc=dc: v.scalar_tensor_tensor(
                    out=yn[:, dc, :T], in0=h[:, dc, :T], scalar=fg[:, dc:dc + 1], in1=rstd[:, :T], op0=ALU.mult,
                    op1=ALU.mult), reads=(("h", dc), ("rstd",), ("pk",)), writes=(("B:yn", dc),))
            for (c0, TB, sg, b) in tblocks(tl):
                for q in range(2):
                    ysb, ysk = YS.next()
                    for hb in range(2):
                        bk, bkey = PS.next()
                        P.pe(lambda t, bk=bk, q=q, hb=hb, c0=c0, TB=TB: [
                            t.transpose(bk[0:TB, j * 128:(j + 1) * 128], yn[:, 8 * q + 4 * hb + j, c0:c0 + TB], ident)
                            for j in range(4)][-1],
                            reads=[("B:yn", 8 * q + 4 * hb + j) for j in range(4)] + [("ident",)], writes=(bkey,))
                        P.act(lambda a, bk=bk, ysb=ysb, hb=hb, TB=TB: a.copy(out=ysb[0:TB, hb * 512:(hb + 1) * 512], in_=bk[0:TB, :]),
                              reads=(bkey,), writes=(ysk,))
                    P.dma("sp", sg["y"][b * TB:(b + 1) * TB, q * 1024:(q + 1) * 1024], ysb[0:TB, :], f"y{ysk[1]}",
                          reads=(ysk,))

        def run_tile(tl):
            tl["ssq_n"] = 0
            tl["T"] = max(sg["c0"] + sg["n"] for sg in tl["segs"])
            for sg in tl["segs"]:
                sg["alive"] = True
            load_x(tl)
            ffn(tl, 0, 0)
            mixer_ab(tl)
            ffn(tl, 0, 1)
            ffn(tl, 1, 0)
            mixer_c(tl)
            if not live(tl):
                return
            ffn(tl, 1, 1)
            final(tl)
            for sg in live(tl):
                if sg["last"]:
                    s = sg["s"]
                    out_state(convhist[s], 8, 30, ncst, ncp if s == 0 else ncs, [("ch", s, cc) for cc in range(8)])
                    out_state(poolhist[s], 16, 15, npst, npp if s == 0 else nps, [("ph", s, g) for g in range(4)])

        P0 = 256
        rem = TOK - P0
        if rem >= 768:
            assert (rem - 768) % 512 == 0
            sizes = [512] * ((rem - 768) // 512) + [384, 384]
        else:
            sizes = [rem] if rem > 0 else []
        seg_h = dict(kind="halo", s=0, c0=P0 + 64, n=128, x=xh, y=None, first=False, last=False)
        seg_s = dict(kind="sample", s=1, c0=P0, n=64, x=xsm, y=ysm, first=False, last=True)
        seg_p = dict(kind="prompt", s=0, c0=0, n=P0, x=xp[0:P0, :], y=yp[0:P0, :], first=True, last=(rem == 0))
        run_tile(dict(segs=[seg_h, seg_s, seg_p]))
        pos = P0
        for it, n in enumerate(sizes):
            run_tile(dict(segs=[dict(kind="prompt", s=0, c0=0, n=n, x=xp[pos:pos + n, :], y=yp[pos:pos + n, :],
                                     first=False, last=(it == len(sizes) - 1))]))
            pos += n

        sem_names = [s for s in P.cnt]
        sems = {}
        for sname in sem_names:
            sems[sname] = es.enter_context(nc.semaphore(f"sem_{sname}"))
        block = es.enter_context(nc.Block())
        P.emit(nc, block, sems)
    return nc


def _fm(v, chunks):
    return np.ascontiguousarray(np.asarray(v, np.float32).reshape(chunks, 128).T)


def make_in_maps(inp, NT):
    SEQH = NT * 512
    f32 = np.float32
    import os as _os
    if _os.environ.get("MK_TINYW", "") == "1":
        shared = dict(
            wsm=np.ascontiguousarray(inp["ffn_w_gate"][0, 0], f32),
            aws=np.ascontiguousarray(inp["a_ws"][0], f32),
            abs=np.ascontiguousarray(np.asarray(inp["a_bs"][0], f32).reshape(1, 512)),
            rows=np.ascontiguousarray(np.concatenate([inp["a_v_norm_g"][0], inp["a_v_norm_b"][0]]).reshape(1, 2048), f32),
        )
    else:
      shared = dict(
        ada_w=np.ascontiguousarray(inp["ada_w"], f32),
        wg=np.ascontiguousarray(inp["ffn_w_gate"], f32), wu=np.ascontiguousarray(inp["ffn_w_up"], f32),
        wd=np.ascontiguousarray(inp["ffn_w_down"], f32), win=np.ascontiguousarray(inp["ab_w_in"], f32),
        wout=np.ascontiguousarray(inp["ab_w_out"], f32), cw=np.ascontiguousarray(inp["c_w_grp"], f32),
        aws=np.ascontiguousarray(inp["a_ws"][0], f32),
        abs=np.ascontiguousarray(np.asarray(inp["a_bs"][0], f32).reshape(1, 512)),
        rows=np.ascontiguousarray(np.concatenate([inp["a_v_norm_g"][0], inp["a_v_norm_b"][0]]).reshape(1, 2048), f32),
    )
    pk_sh = np.zeros((128, NPK), f32)

    def put(pkarr, name, val):
        o, w = PK[name]
        pkarr[:, o:o + w] = np.asarray(val, f32).reshape(128, w)

    put(pk_sh, "ada_b", np.concatenate([_fm(inp["ada_b"][l], 144) for l in range(2)], axis=1))
    put(pk_sh, "norm_g", np.concatenate([_fm(inp["norm_g"][l, i], 16) for l in range(2) for i in range(3)], axis=1))
    put(pk_sh, "final_g", _fm(inp["final_norm_g"], 16))
    put(pk_sh, "dw", np.asarray(inp["b_dw"][0], f32).T.reshape(8, 128, 31).transpose(1, 0, 2))
    put(pk_sh, "ln_g", _fm(inp["b_ln_g"][0], 8))
    put(pk_sh, "ln_b", _fm(inp["b_ln_b"][0], 8))
    put(pk_sh, "c_b", _fm(np.asarray(inp["c_b_grp"][0]).reshape(-1), 16))
    put(pk_sh, "c_scale", _fm(inp["c_scale"][0], 16))
    maps = []
    for c in range(NCORES):
        bp, half, bs = c // 2, c % 2, c
        m = dict(shared)
        m["xp"] = np.ascontiguousarray(inp["x_prompt"][bp, half * SEQH:(half + 1) * SEQH], f32)
        m["xh"] = (np.ascontiguousarray(inp["x_prompt"][bp, SEQH - 128:SEQH], f32) if half == 1
                   else np.zeros((128, D), f32))
        m["xsm"] = np.ascontiguousarray(inp["x_sample"][bs], f32)
        pkc = pk_sh.copy()
        c2 = np.stack([np.asarray(inp["c_prompt"][bp], f32), np.asarray(inp["c_sample"][bs], f32)], axis=-1)
        put(pkc, "c2T", c2.reshape(16, 128, 2).transpose(1, 0, 2))
        put(pkc, "sconv", np.asarray(inp["state_conv_b"][0, bs], f32).T.reshape(8, 128, 30).transpose(1, 0, 2))
        put(pkc, "spool", np.asarray(inp["state_pool_c"][0, bs], f32).T.reshape(16, 128, 15).transpose(1, 0, 2))
        put(pkc, "flag", np.full((128, 1), float(half), f32))
        pos = half * SEQH + np.arange(16)
        iv = np.stack([1.0 / np.minimum(2 ** (g + 1), pos + 1) for g in range(4)]).astype(f32)
        put(pkc, "invc", np.broadcast_to(iv[None, :, None, :], (128, 4, 4, 16)))
        m["pk"] = pkc
        maps.append(m)
    return maps


_NC_CACHE = {}


def run(inp, NT):
    if NT not in _NC_CACHE:
        _NC_CACHE[NT] = build(NT)
    nc = _NC_CACHE[NT]
    maps = make_in_maps(inp, NT)
    res = run_bass_kernel_spmd(nc, maps, core_ids=list(range(NCORES)))
    R = res.results
    SEQH = NT * 512
    y_prompt = np.stack([np.concatenate([R[2 * b]["yp"], R[2 * b + 1]["yp"]], axis=0) for b in range(4)])
    y_sample = np.stack([R[c]["ysm"] for c in range(8)])
    ncp = np.stack([R[2 * b + 1]["ncp"] for b in range(4)])[None]
    ncs = np.stack([R[c]["ncs"] for c in range(8)])[None]
    npp = np.stack([R[2 * b + 1]["npp"] for b in range(4)])[None]
    nps = np.stack([R[c]["nps"] for c in range(8)])[None]
    nav = np.stack([R[c]["nav"] for c in range(8)])[None]
    return tuple(np.ascontiguousarray(a, dtype=np.float32) for a in (y_prompt, y_sample, ncp, ncs, npp, nps, nav))


def kernel(**inputs):
    inp = {k: np.asarray(v) for k, v in inputs.items()}
    return run(inp, 8)
```
